# Optimizing a Trainium2 kernel written in Bass

```python
import jax, jax.numpy as jnp
from jax import lax
import numpy as np

D_MODEL = 2048
BATCH = 2
SEQ = 16384
DEPTH = 1

GRID_W = 64
MIX_DIM = D_MODEL
CONV_DIM = MIX_DIM // 2
NA_HEADS = 16
NA_HEAD_DIM = 64
NA_DIM = NA_HEADS * NA_HEAD_DIM
CONV_K = 31
WIN_H = 8
WIN_W = 16
IN_COLS = 2 * CONV_DIM + 3 * NA_DIM
D_FF = 5632
FFN_K = 3
EPS = 1e-6

kernel_name = "hybrid_conformer_conv_natten_convffn"


def rms_norm(x, g):
    xf = x.astype(jnp.float32)
    y = xf * lax.rsqrt(jnp.mean(xf * xf, axis=-1, keepdims=True) + EPS)
    return (y * g.astype(jnp.float32)).astype(x.dtype)


def layer_norm(x, g, b):
    xf = x.astype(jnp.float32)
    mu = jnp.mean(xf, axis=-1, keepdims=True)
    var = jnp.mean(jnp.square(xf - mu), axis=-1, keepdims=True)
    y = (xf - mu) * lax.rsqrt(var + EPS)
    return (y * g.astype(jnp.float32) + b.astype(jnp.float32)).astype(x.dtype)


def depthwise_conv(x, w, b):
    k = w.shape[0]
    c = x.shape[-1]
    y = lax.conv_general_dilated(
        x, w[:, None, :].astype(x.dtype), window_strides=(1,),
        padding=[(k // 2, k // 2)], dimension_numbers=("NWC", "WIO", "NWC"),
        feature_group_count=c)
    return y + b.astype(x.dtype)


def conformer_conv(u_val, u_gate, dw_w, dw_b, ln_g, ln_b):
    h = u_val * jax.nn.sigmoid(u_gate)
    h = depthwise_conv(h, dw_w, dw_b)
    h = layer_norm(h, ln_g, ln_b)
    return jax.nn.silu(h)


def neighborhood_attention(q, k, v, rpb):
    b, s, h, dh = q.shape
    rows = s // GRID_W
    kh = min(WIN_H, rows)
    kw = WIN_W
    qg = q.reshape(b, rows, GRID_W, h, dh) * (dh ** -0.5)
    kg = k.reshape(b, rows, GRID_W, h, dh)
    vg = v.reshape(b, rows, GRID_W, h, dh)
    cols = np.arange(GRID_W)
    col_start = np.clip(cols - kw // 2, 0, GRID_W - kw)
    col_idx = col_start[:, None] + np.arange(kw)[None, :]
    dc = col_idx - cols[:, None] + (WIN_W - 1)
    rpb_cols = rpb[:, :, dc].astype(jnp.float32)

    def row_block(r):
        rs = jnp.clip(r - kh // 2, 0, rows - kh)
        k_rows = lax.dynamic_slice_in_dim(kg, rs, kh, axis=1)
        v_rows = lax.dynamic_slice_in_dim(vg, rs, kh, axis=1)
        k_win = k_rows[:, :, col_idx]
        v_win = v_rows[:, :, col_idx]
        q_row = lax.dynamic_index_in_dim(qg, r, axis=1, keepdims=False)
        scores = jnp.einsum("bchd,bicjhd->bchij", q_row, k_win).astype(jnp.float32)
        dr = rs + jnp.arange(kh) - r + (WIN_H - 1)
        bias = rpb_cols[:, dr]
        scores = scores + jnp.transpose(bias, (2, 0, 1, 3))[None]
        p = jax.nn.softmax(scores.reshape(b, GRID_W, h, kh * kw), axis=-1)
        p = p.reshape(b, GRID_W, h, kh, kw).astype(v.dtype)
        return jnp.einsum("bchij,bicjhd->bchd", p, v_win)

    out = lax.map(row_block, jnp.arange(rows))
    return jnp.transpose(out, (1, 0, 2, 3, 4)).reshape(b, s, h * dh)


def setup_inputs(seed: int = 0) -> dict:
    key = jax.random.key(seed)
    ks = jax.random.split(key, 18)
    L = DEPTH

    def nrm(k, shape, scale):
        return jax.random.normal(k, shape, jnp.float32) * scale

    return {
        "x": nrm(ks[0], (BATCH, SEQ, D_MODEL), 1.0),
        "attn_norm_g": 1.0 + nrm(ks[1], (L, D_MODEL), 0.02),
        "w_in": nrm(ks[2], (L, D_MODEL, IN_COLS), D_MODEL ** -0.5),
        "conv_dw_w": nrm(ks[3], (L, CONV_K, CONV_DIM), CONV_K ** -0.5),
        "conv_dw_b": nrm(ks[4], (L, CONV_DIM), 0.02),
        "conv_ln_g": 1.0 + nrm(ks[5], (L, CONV_DIM), 0.02),
        "conv_ln_b": nrm(ks[6], (L, CONV_DIM), 0.02),
        "rpb": nrm(ks[7], (L, NA_HEADS, 2 * WIN_H - 1, 2 * WIN_W - 1), 0.1),
        "conv_out_g": 1.0 + nrm(ks[8], (L, CONV_DIM), 0.02),
        "na_out_g": 1.0 + nrm(ks[9], (L, NA_DIM), 0.02),
        "w_out": nrm(ks[10], (L, MIX_DIM, D_MODEL), MIX_DIM ** -0.5),
        "ffn_norm_g": 1.0 + nrm(ks[11], (L, D_MODEL), 0.02),
        "w_up": nrm(ks[12], (L, D_MODEL, 2 * D_FF), D_MODEL ** -0.5),
        "ffn_dw_w": nrm(ks[13], (L, FFN_K, 2 * D_FF), FFN_K ** -0.5),
        "ffn_dw_b": nrm(ks[14], (L, 2 * D_FF), 0.02),
        "w_down": nrm(ks[15], (L, D_FF, D_MODEL), D_FF ** -0.5),
        "final_norm_g": 1.0 + nrm(ks[16], (D_MODEL,), 0.02),
    }


def reference(x, attn_norm_g, w_in, conv_dw_w, conv_dw_b, conv_ln_g, conv_ln_b,
              rpb, conv_out_g, na_out_g, w_out, ffn_norm_g, w_up, ffn_dw_w,
              ffn_dw_b, w_down, final_norm_g):
    b, s, _ = x.shape
    split_pts = [CONV_DIM, 2 * CONV_DIM, 2 * CONV_DIM + NA_DIM, 2 * CONV_DIM + 2 * NA_DIM]
    for l in range(DEPTH):
        xn = rms_norm(x, attn_norm_g[l])
        proj = jnp.einsum("bsd,dc->bsc", xn, w_in[l])
        u_val, u_gate, q, k, v = jnp.split(proj, split_pts, axis=-1)
        y_conv = conformer_conv(u_val, u_gate, conv_dw_w[l], conv_dw_b[l],
                                conv_ln_g[l], conv_ln_b[l])
        heads = lambda t: t.reshape(b, s, NA_HEADS, NA_HEAD_DIM)
        y_na = neighborhood_attention(heads(q), heads(k), heads(v), rpb[l])
        mixed = jnp.concatenate([rms_norm(y_conv, conv_out_g[l]),
                                 rms_norm(y_na, na_out_g[l])], axis=-1)
        x = x + jnp.einsum("bsc,cd->bsd", mixed, w_out[l])
        xn = rms_norm(x, ffn_norm_g[l])
        up = jnp.einsum("bsd,df->bsf", xn, w_up[l])
        up = depthwise_conv(up, ffn_dw_w[l], ffn_dw_b[l])
        gate, val = jnp.split(up, 2, axis=-1)
        x = x + jnp.einsum("bsf,fd->bsd", jax.nn.silu(gate) * val, w_down[l])
    return rms_norm(x, final_norm_g)
```

```python
import contextlib
import os
import numpy as np
import concourse.bass as bass
import concourse.mybir as mybir
from concourse.bass_utils import run_bass_kernel_spmd

F32 = mybir.dt.float32
BF16 = mybir.dt.bfloat16
AF = mybir.ActivationFunctionType
ALU = mybir.AluOpType

ENGS = ("pe", "act", "dve", "pool", "sp")
EPOCH = 30000

D = 2048
SEQ = 16384
NCORE = 8
TPC = 4096
NT1 = 5120
OFF1 = 512
NT2 = 4352
OFF2 = 128
DFF = 5632
EPS = 1e-6
NEG = -30000.0
P3W = [456] * 8 + [448]
DBG = {}


class Prog:
    def __init__(self, nc, stack):
        self.nc = nc
        self.stack = stack
        self.q = {e: [] for e in ENGS}
        self.cnt = {e: 0 for e in ENGS}
        self.csem = {}
        self.waited = {e: {} for e in ENGS}
        self.bufs = {}
        self.slots = {}
        self.nsem = 0
        for e in ("pe", "act", "dve", "pool"):
            self._new_csem(e)

    def sem(self, name):
        self.nsem += 1
        return self.stack.enter_context(self.nc.semaphore(f"{name}_{self.nsem}"))

    def _new_csem(self, e):
        self.csem[e] = self.sem("c" + e)
        self.cnt[e] = 0

    def _wait(self, e, tok):
        s, v = tok
        w = self.waited[e]
        if w.get(id(s), 0) >= v:
            return
        w[id(s)] = v
        self.q[e].append(("wait", s, v))

    def _deps(self, e, reads, writes):
        for k in reads:
            b = self.bufs.get(k)
            if b and b[0] is not None:
                self._wait(e, b[0])
        for k in writes:
            b = self.bufs.get(k)
            if b:
                if b[0] is not None:
                    self._wait(e, b[0])
                for t in b[1].values():
                    self._wait(e, t)

    def _mark(self, tok, reads, writes):
        for k in reads:
            b = self.bufs.setdefault(k, [None, {}])
            o = b[1].get(id(tok[0]))
            if o is None or o[1] < tok[1]:
                b[1][id(tok[0])] = tok
        for k in writes:
            self.bufs[k] = [tok, {}]

    def op(self, e, fn, reads=(), writes=()):
        if self.cnt[e] >= EPOCH:
            self._new_csem(e)
        self._deps(e, reads, writes)
        self.cnt[e] += 1
        tok = (self.csem[e], self.cnt[e])
        self.q[e].append(("op", fn, self.csem[e]))
        self._mark(tok, reads, writes)
        return tok

    def dma(self, e, slot, fn, reads=(), writes=()):
        if slot not in self.slots:
            self.slots[slot] = [self.sem("d"), 0]
        sl = self.slots[slot]
        if sl[1] >= EPOCH:
            sl[0] = self.sem("d")
            sl[1] = 0
        self._deps(e, reads, writes)
        sl[1] += 16
        tok = (sl[0], sl[1])
        self.q[e].append(("dma", fn, sl[0]))
        self._mark(tok, reads, writes)
        return tok

    def wait(self, e, tok):
        self._wait(e, tok)

    def barrier(self, dram_slots=()):
        toks = [(self.csem[e], self.cnt[e]) for e in ("pe", "act", "dve", "pool") if self.cnt[e] > 0]
        toks += [(s[0], s[1]) for k, s in self.slots.items() if s[1] > 0 and k not in dram_slots]
        for e in ENGS:
            for t in toks:
                self._wait(e, t)

    def replay(self):
        engobj = {"pe": "tensor", "act": "scalar", "dve": "vector", "pool": "gpsimd", "sp": "sync"}
        with self.nc.Block() as block:
            for e in ENGS:
                items = self.q[e]
                if not items:
                    continue

                def body(eng, items=items):
                    for it in items:
                        if it[0] == "wait":
                            eng.wait_ge(it[1], it[2])
                        elif it[0] == "op":
                            it[1](eng).then_inc(it[2], 1)
                        else:
                            it[1](eng).then_inc(it[2], 16)

                getattr(block, engobj[e])(body)


def build(phases=("p0", "p1", "p2a", "p2b", "p3"), debug=False):
    nc = bass.Bass("TRN2", target_bir_lowering=False)
    dbgset = set(debug) if debug else set()

    def din(name, shape, dt=F32):
        return nc.dram_tensor(name, shape, dt, kind="ExternalInput").ap()

    PROD = {"win16": "p0", "wout16": "p0", "wup16": "p0", "wdn16": "p0", "hS": "p1", "qS": "p1", "kS": "p1", "vS": "p1",
            "mcS": "p2a", "x1S": "p2b", "xn2S": "p2b"}

    def dscr(name, shape, dt, force_in=False):
        if PROD[name] not in phases:
            return nc.dram_tensor(name, shape, dt, kind="ExternalInput").ap()
        if name in dbgset:
            return nc.dram_tensor(name, shape, dt, kind="ExternalOutput").ap()
        return nc.dram_tensor(name, shape, dt).ap()

    xT = din("xT", [D, NT1])
    win = din("win", [40, 128, 16, 128])
    wout = din("wout", [16, 128, 16, 128])
    wup = din("wup", [44, 128, 16, 256])
    wdn = din("wdn", [16, 128, 44, 128])
    g_attn = din("g_attn", [128, 16])
    cdw = din("cdw", [128, 8, 31])
    cdb = din("cdb", [128, 8])
    clg = din("clg", [128, 8])
    clb = din("clb", [128, 8])
    cog = din("cog", [128, 8])
    gna = din("gna", [128, 1024])
    g_ffn = din("g_ffn", [128, 16])
    fdw = din("fdw", [128, 88, 3])
    fdb = din("fdb", [128, 88])
    g_fin = din("g_fin", [128, 16])
    btab = din("btab", [5, 128, 16, 768])
    tmask = din("tmask", [128, NT2])
    ident_in = din("ident_in", [128, 128])
    yT = nc.dram_tensor("yT", [D, TPC], F32, kind="ExternalOutput").ap()

    win16 = dscr("win16", [40, 128, 16, 128], BF16)
    wout16 = dscr("wout16", [16, 128, 16, 128], BF16)
    wup16 = dscr("wup16", [44, 128, 16, 256], BF16)
    wdn16 = dscr("wdn16", [16, 128, 44, 128], BF16)
    hS = dscr("hS", [8, 128, NT1], BF16)
    qS = dscr("qS", [8, 128, NT1], BF16)
    kS = dscr("kS", [8, 128, NT1], BF16)
    vS = dscr("vS", [NT1, 1040], BF16)
    mcS = dscr("mcS", [8, 128, NT2], BF16)
    x1S = dscr("x1S", [16, 128, NT2], F32)
    xn2S = dscr("xn2S", [16, 128, NT2], BF16)

    xT3 = xT.rearrange("(c p) t -> p c t", p=128)

    with contextlib.ExitStack() as top:
        P = Prog(nc, top)

        def sb(st, name, shape, dt):
            return st.enter_context(nc.sbuf_tensor(name, shape, dt))

        def ps(st, name, shape, dt=F32):
            return st.enter_context(nc.psum_tensor(name, shape, dt))

        ones_bf = sb(top, "ones_bf", [128, 128], BF16)
        ones_f = sb(top, "ones_f", [128, 128], F32)
        ident = sb(top, "ident", [128, 128], BF16)
        identf = sb(top, "identf", [128, 128], F32)
        P.op("dve", lambda e: e.memset(ones_bf[:], 1.0), writes=["ones_bf"])
        P.op("dve", lambda e: e.memset(ones_f[:], 1.0), writes=["ones_f"])
        P.dma("sp", "identf", lambda e: e.dma_start(out=identf[:], in_=ident_in), writes=["identf"])
        P.op("dve", lambda e: e.tensor_copy(out=ident[:], in_=identf[:]), reads=["identf"], writes=["ident"])

        if "p0" in phases:
            def cast(dst, src, rows, step, key):
                d2 = dst.rearrange("a p k m -> (a p) (k m)")
                s2 = src.rearrange("a p k m -> (a p) (k m)")
                tok = None
                for r in range(0, rows, step):
                    tok = P.dma("pool", "cast_" + key, lambda e, r=r: e.dma_start(out=d2[r:r + step, :], in_=s2[r:r + step, :]))
                for a in range(rows // 128):
                    P.bufs[(key, a)] = [tok, {}]
            cast(win16, win, 40 * 128, 512, "win16")
            class ChunkCast:
                def __init__(self, dst, src, n, key):
                    self.d2 = dst.rearrange("a p k m -> (a p) (k m)")
                    self.s2 = src.rearrange("a p k m -> (a p) (k m)")
                    self.n, self.key, self.i, self.tok = n, key, 0, None

                def emit(self, cnt=1):
                    for _ in range(cnt):
                        if self.i >= self.n:
                            return
                        r = self.i * 128
                        self.tok = P.dma("pool", "cast_" + self.key, lambda e, r=r: e.dma_start(out=self.d2[r:r + 128, :], in_=self.s2[r:r + 128, :]))
                        self.i += 1
                        if self.i == self.n:
                            for a in range(self.n):
                                P.bufs[(self.key, a)] = [self.tok, {}]

                def flush(self):
                    self.emit(self.n)
            cc_up = ChunkCast(wup16, wup, 44, "wup16")
            cc_dn = ChunkCast(wdn16, wdn, 16, "wdn16")
            late_casts = {2: lambda: cast(wout16, wout, 16 * 128, 512, "wout16")}
            if not all(p in phases for p in ("p1", "p2a", "p2b")):
                for k in list(late_casts):
                    late_casts[k]()
                late_casts = {}
                cc_up.flush()
                cc_dn.flush()

        if "p1" in phases:
            with contextlib.ExitStack() as st:
                xs = [sb(st, f"p1x{i}", [128, 16, 512], F32) for i in range(2)]
                xn = [sb(st, f"p1xn{i}", [128, 16, 512], BF16) for i in range(2)]
                sq = [sb(st, f"p1sq{i}", [128, 512], BF16) for i in range(2)]
                NWB = 5
                wb = [sb(st, f"p1w{i}", [128, 16, 128], BF16) for i in range(NWB)]
                wv = [sb(st, f"p1wv{i}", [128, 4, 16, 128], BF16) for i in range(2)]
                wst = [sb(st, f"p1wst{i}", [128, 16, 128], F32) for i in range(3)]
                NDIRECT = DBG.get("p1_direct", 2)
                wsc = [0]
                rstd = sb(st, "p1rstd", [128, 512], F32)
                gat = sb(st, "p1g", [128, 16], F32)
                sig = [sb(st, f"p1sig{i}", [128, 512], F32) for i in range(2)]
                hst = [sb(st, f"p1h{i}", [128, 512], BF16) for i in range(2)]
                qst = [sb(st, f"p1q{i}", [128, 512], BF16) for i in range(3)]
                vst = [sb(st, f"p1v{i}", [128, 16, 65], BF16) for i in range(2)]
                pA = [ps(st, f"p1pA{i}", [128, 512]) for i in range(2)]
                pB = [ps(st, f"p1pB{i}", [128, 512]) for i in range(2)]
                pS = ps(st, "p1pS", [128, 512])
                P.dma("sp", "p1g", lambda e: e.dma_start(out=gat[:], in_=g_attn), writes=["p1g"])
                for i in range(2):
                    P.op("pool", lambda e, i=i: e.memset(vst[i][:], 1.0), writes=[("vst", i)])
                wcnt = [0]
                NTILE = NT1 // 512
                def p1_norm(ti):
                    a = ti * 512
                    b2 = ti % 2
                    P.dma("sp", f"p1x{b2}", lambda e, a=a, b2=b2: e.dma_start(out=xs[b2][:], in_=xT3[:, :, a:a + 512]),
                          writes=[("xs", b2)])
                    for c in range(16):
                        s2 = c % 2
                        P.op("act", lambda e, c=c, s2=s2, b2=b2: e.activation(out=sq[s2][:], in_=xs[b2][:, c, :], func=AF.Square),
                             reads=[("xs", b2)], writes=[("sq", s2)])
                        P.op("pe", lambda e, c=c, s2=s2: e.matmul(pS[:], lhsT=ones_bf[:], rhs=sq[s2][:], start=(c == 0), stop=(c == 15)),
                             reads=[("sq", s2), "ones_bf"], writes=["pS"])
                    P.op("act", lambda e: e.activation(out=rstd[:], in_=pS[:], func=AF.Sqrt, scale=1.0 / D, bias=EPS),
                         reads=["pS"], writes=["rstd"])
                    P.op("dve", lambda e: e.reciprocal(out=rstd[:], in_=rstd[:]), reads=["rstd"], writes=["rstd"])
                    for c in range(16):
                        eng = "dve"
                        P.op(eng, lambda e, c=c, b2=b2: e.scalar_tensor_tensor(
                            out=xn[b2][:, c, :], in0=xs[b2][:, c, :], scalar=gat[:, c:c + 1], in1=rstd[:],
                            op0=ALU.mult, op1=ALU.mult),
                            reads=[("xs", b2), "rstd", "p1g"], writes=[("xn", b2, c)])

                def p1_tile(ti):
                    a = ti * 512
                    b2 = ti % 2
                    xnr = [("xn", b2, c) for c in range(16)]

                    def load_w(cc):
                        w = wcnt[0] % NWB
                        wcnt[0] += 1
                        if ti < NDIRECT:
                            k = wsc[0] % 3
                            wsc[0] += 1
                            P.dma("sp", f"p1wst{k}", lambda e, cc=cc, k=k: e.dma_start(out=wst[k][:], in_=win[cc]), writes=[("wst", k)])
                            eng = "pool" if wsc[0] % 2 == 0 else "dve"
                            P.op(eng, lambda e, k=k, w=w: e.tensor_copy(out=wb[w][:], in_=wst[k][:]), reads=[("wst", k)], writes=[("wb", w)])
                            return w
                        P.dma("sp", f"p1w{w}", lambda e, cc=cc, w=w: e.dma_start(out=wb[w][:], in_=win16[cc]),
                              reads=[("win16", cc)], writes=[("wb", w)])
                        return w

                    def mm_fm(psum, pkey, w):
                        def f(e):
                            for c in range(16):
                                r = e.matmul(psum[:], lhsT=wb[w][:, c, :], rhs=xn[b2][:, c, :], start=(c == 0), stop=(c == 15))
                            return r
                        P.op("pe", f, reads=[("wb", w)] + xnr, writes=[pkey])

                    def p1_u(j):
                        pb = j % 2
                        w1 = load_w(j)
                        w2 = load_w(8 + j)
                        mm_fm(pA[pb], ("pA", pb), w1)
                        mm_fm(pB[pb], ("pB", pb), w2)
                        P.op("act", lambda e, pb=pb: e.activation(out=sig[pb][:], in_=pB[pb][:], func=AF.Sigmoid),
                             reads=[("pB", pb)], writes=[("sig", pb)])
                        P.op("dve", lambda e, pb=pb: e.tensor_tensor(out=hst[pb][:], in0=pA[pb][:], in1=sig[pb][:], op=ALU.mult),
                             reads=[("pA", pb), ("sig", pb)], writes=[("hst", pb)])
                        P.dma("pool", f"p1h{pb}", lambda e, j=j, pb=pb, a=a: e.dma_start(out=hS[j, :, a:a + 512], in_=hst[pb][:]),
                              reads=[("hst", pb)], writes=[("hS", j, ti)])
                    for j in range(8):
                        p1_u(j)
                    if ti + 1 < DBG.get("p1_tiles", NTILE):
                        p1_norm(ti + 1)
                    def p1_qk(j):
                        pb = j % 2
                        w1 = load_w(16 + j)
                        mm_fm(pA[pb], ("pA", pb), w1)
                        qb = j % 3
                        if j < 8:
                            P.op("act", lambda e, pb=pb, qb=qb: e.activation(out=qst[qb][:], in_=pA[pb][:], func=AF.Copy, scale=0.125),
                                 reads=[("pA", pb)], writes=[("qst", qb)])
                            P.dma("pool", f"p1q{qb}", lambda e, j=j, qb=qb, a=a: e.dma_start(out=qS[j, :, a:a + 512], in_=qst[qb][:]),
                                  reads=[("qst", qb)], writes=[("qS", j, ti)])
                        else:
                            P.op("dve", lambda e, pb=pb, qb=qb: e.tensor_copy(out=qst[qb][:], in_=pA[pb][:]),
                                 reads=[("pA", pb)], writes=[("qst", qb)])
                            P.dma("pool", f"p1q{qb}", lambda e, j=j, qb=qb, a=a: e.dma_start(out=kS[j - 8, :, a:a + 512], in_=qst[qb][:]),
                                  reads=[("qst", qb)], writes=[("kS", j - 8, ti)])
                    for j in range(16):
                        p1_qk(j)
                    for hb in range(2):
                        if ti < NDIRECT:
                            for a4 in range(4):
                                k = wsc[0] % 3
                                wsc[0] += 1
                                P.dma("sp", f"p1wst{k}", lambda e, hb=hb, a4=a4, k=k: e.dma_start(out=wst[k][:], in_=win[32 + 4 * hb + a4]), writes=[("wst", k)])
                                eng = "pool" if wsc[0] % 2 == 0 else "dve"
                                P.op(eng, lambda e, hb=hb, a4=a4, k=k: e.tensor_copy(out=wv[hb][:, a4, :, :], in_=wst[k][:]),
                                     reads=[("wst", k)], writes=[("wv", hb)])
                            continue
                        P.dma("sp", f"p1wv{hb}", lambda e, hb=hb: e.dma_start(
                            out=wv[hb][:], in_=win16[32 + 4 * hb:36 + 4 * hb].rearrange("a p k m -> p a k m")),
                            reads=[("win16", 32 + 4 * hb + i) for i in range(4)], writes=[("wv", hb)])
                    def p1_v(tsub):
                      vb = tsub % 2
                      for hb in range(2):
                            pb = (tsub * 2 + hb) % 2

                            def f(e, hb=hb, tsub=tsub, pb=pb):
                                for c in range(16):
                                    r = e.matmul(pB[pb][:], lhsT=xn[b2][:, c, tsub * 128:(tsub + 1) * 128],
                                                 rhs=wv[hb][:, :, c, :], start=(c == 0), stop=(c == 15))
                                return r
                            P.op("pe", f, reads=[("wv", hb)] + xnr, writes=[("pB", pb)])
                            eng = "act" if hb == 0 else "dve"
                            if eng == "act":
                                P.op("act", lambda e, hb=hb, pb=pb, vb=vb: e.activation(
                                    out=vst[vb][:, hb * 8:(hb + 1) * 8, 0:64],
                                    in_=pB[pb][:].rearrange("p (h d) -> p h d", d=64), func=AF.Copy),
                                    reads=[("pB", pb)], writes=[("vst", vb)])
                            else:
                                P.op("dve", lambda e, hb=hb, pb=pb, vb=vb: e.tensor_copy(
                                    out=vst[vb][:, hb * 8:(hb + 1) * 8, 0:64],
                                    in_=pB[pb][:].rearrange("p (h d) -> p h d", d=64)),
                                    reads=[("pB", pb)], writes=[("vst", vb)])
                      P.dma("pool", f"p1v{vb}", lambda e, tsub=tsub, vb=vb, a=a: e.dma_start(
                            out=vS[a + tsub * 128:a + (tsub + 1) * 128, :], in_=vst[vb][:].rearrange("p h d -> p (h d)")),
                            reads=[("vst", vb)], writes=[("vS", ti)])
                    for tsub in range(4):
                        p1_v(tsub)
                p1_norm(0)
                for ti in range(DBG.get("p1_tiles", NTILE)):
                    p1_tile(ti)
                    if "p0" in phases and ti in late_casts:
                        late_casts.pop(ti)()
                if "p0" in phases and 2 in late_casts:
                    late_casts.pop(2)()
                P.barrier(dram_slots=("cast_win16", "cast_wout16", "cast_wup16", "cast_wdn16"))

        if "p2a" in phases:
            with contextlib.ExitStack() as st:
                hb_ = [sb(st, f"ah{i}", [128, 8, 286], BF16) for i in range(2)]
                accA2 = [sb(st, f"aaccA{i}", [128, 8, 256], F32) for i in range(2)]
                diagw = sb(st, "adiag", [128, 8, 31, 128], BF16)
                pcv = [ps(st, f"apcv{i}", [128, 512]) for i in range(4)]
                sqt = [sb(st, f"asq{i}", [128, 256], BF16) for i in range(2)]
                mean = sb(st, "amean", [128, 256], F32)
                msq = sb(st, "amsq", [128, 256], F32)
                rs = sb(st, "ars", [128, 256], F32)
                nmr = sb(st, "anmr", [128, 256], F32)
                rs2 = sb(st, "ars2", [128, 256], F32)
                yc2 = [sb(st, f"ayc{i}", [128, 8, 256], F32) for i in range(2)]
                mo = [sb(st, f"amo{i}", [128, 8, 256], BF16) for i in range(2)]
                w_dw = sb(st, "awdw", [128, 8, 31], F32)
                b_dw = sb(st, "abdw", [128, 8], F32)
                lg = sb(st, "alg", [128, 8], F32)
                lb = sb(st, "alb", [128, 8], F32)
                og = sb(st, "aog", [128, 8], F32)
                pSt = ps(st, "apst", [128, 512])
                pS2 = ps(st, "aps2", [128, 256])
                for t_, src, key in ((w_dw, cdw, "awdw"), (b_dw, cdb, "abdw"), (lg, clg, "alg"), (lb, clb, "alb"), (og, cog, "aog")):
                    P.dma("sp", key, lambda e, t_=t_, src=src: e.dma_start(out=t_[:], in_=src), writes=[key])
                par = ["awdw", "abdw", "alg", "alb", "aog"]
                for c in range(8):
                    for k in range(31):
                        if k % 2 == 0:
                            P.op("act", lambda e, c=c, k=k: e.activation(out=diagw[:, c, k, :], in_=identf[:], func=AF.Copy, scale=w_dw[:, c, k:k + 1]),
                                 reads=["identf", "awdw"], writes=[("adiag", c, 0)])
                        else:
                            P.op("dve", lambda e, c=c, k=k: e.scalar_tensor_tensor(out=diagw[:, c, k, :], in0=identf[:], scalar=w_dw[:, c, k:k + 1], in1=identf[:],
                                                                                   op0=ALU.mult, op1=ALU.mult),
                                 reads=["identf", "awdw"], writes=[("adiag", c, 1)])
                def p2a_conv(s):
                    steps = []
                    hb2 = s % 2
                    accA = accA2[hb2]
                    yc = yc2[hb2]
                    a1 = 256 * s + 384 - 15
                    steps.append(lambda: P.dma("sp", f"ah{hb2}", lambda e: e.dma_start(
                        out=hb_[hb2][:], in_=hS.rearrange("c p t -> p c t")[:, :, a1:a1 + 286]),
                        reads=[("hS", j, ti) for j in range(8) for ti in range(max(0, a1 // 512), min(NT1 // 512, (a1 + 285) // 512 + 1))],
                        writes=[("ah", hb2)]))
                    def cstep(c):
                        pb = c % 4

                        def fc(e, c=c, pb=pb, hb2=hb2):
                            for k in range(31):
                                r = e.matmul(pcv[pb][:, 0:256], lhsT=diagw[:, c, k, :], rhs=hb_[hb2][:, c, k:k + 256], start=(k == 0), stop=(k == 30))
                            return r
                        P.op("pe", fc, reads=[("ah", hb2), ("adiag", c, 0), ("adiag", c, 1)], writes=[("apcv", pb)])
                        P.op("act", lambda e, c=c, pb=pb: e.activation(out=accA[:, c, :], in_=pcv[pb][:, 0:256], func=AF.Identity, bias=b_dw[:, c:c + 1]),
                             reads=[("apcv", pb)] + par, writes=[("accA", hb2, c)])
                    for c in range(8):
                        steps.append(lambda c=c: cstep(c))
                    return steps

                class Rec:
                    def __init__(self):
                        self.items = []

                    def op(self, *a, **k):
                        self.items.append(lambda: P.op(*a, **k))

                    def dma(self, *a, **k):
                        self.items.append(lambda: P.dma(*a, **k))

                def p2a_chain(s):
                    R = Rec()
                    hb2 = s % 2
                    accA = accA2[hb2]
                    yc = yc2[hb2]
                    for c in range(8):
                        s2 = c % 2
                        R.op("act", lambda e, c=c, s2=s2: e.activation(out=sqt[s2][:], in_=accA[:, c, :], func=AF.Copy),
                             reads=[("accA", hb2, c)], writes=[("asq", s2)])
                        R.op("pe", lambda e, c=c, s2=s2: e.matmul(pSt[:, 0:256], lhsT=ones_bf[:], rhs=sqt[s2][:], start=(c == 0), stop=(c == 7)),
                             reads=[("asq", s2), "ones_bf"], writes=["apst0"])
                    for c in range(8):
                        s2 = c % 2
                        R.op("act", lambda e, c=c, s2=s2: e.activation(out=sqt[s2][:], in_=accA[:, c, :], func=AF.Square),
                             reads=[("accA", hb2, c)], writes=[("asq", s2)])
                        R.op("pe", lambda e, c=c, s2=s2: e.matmul(pSt[:, 256:512], lhsT=ones_bf[:], rhs=sqt[s2][:], start=(c == 0), stop=(c == 7)),
                             reads=[("asq", s2), "apst0", "ones_bf"], writes=["apst1"])
                    R.op("act", lambda e: e.activation(out=mean[:], in_=pSt[:, 0:256], func=AF.Copy, scale=1.0 / 1024),
                         reads=["apst0", "apst1"], writes=["amean"])
                    R.op("act", lambda e: e.activation(out=rs[:], in_=pSt[:, 256:512], func=AF.Copy, scale=1.0 / 1024),
                         reads=["apst1"], writes=["ars"])
                    R.op("dve", lambda e: e.tensor_tensor(out=msq[:], in0=mean[:], in1=mean[:], op=ALU.mult),
                         reads=["amean"], writes=["amsq"])
                    R.op("dve", lambda e: e.tensor_tensor(out=rs[:], in0=rs[:], in1=msq[:], op=ALU.subtract),
                         reads=["ars", "amsq"], writes=["ars"])
                    R.op("act", lambda e: e.activation(out=rs[:], in_=rs[:], func=AF.Sqrt, bias=EPS), reads=["ars"], writes=["ars"])
                    R.op("dve", lambda e: e.reciprocal(out=rs[:], in_=rs[:]), reads=["ars"], writes=["ars"])
                    R.op("dve", lambda e: e.tensor_tensor(out=nmr[:], in0=mean[:], in1=rs[:], op=ALU.mult),
                         reads=["amean", "ars"], writes=["anmr"])
                    for c in range(8):
                        eng = "dve" if c % 2 == 0 else "pool"
                        R.op(eng, lambda e, c=c: e.tensor_tensor(out=accA[:, c, :], in0=accA[:, c, :], in1=rs[:], op=ALU.mult),
                             reads=[("accA", hb2, c), "ars"], writes=[("accA", hb2, c)])
                        R.op(eng, lambda e, c=c: e.tensor_tensor(out=accA[:, c, :], in0=accA[:, c, :], in1=nmr[:], op=ALU.subtract),
                             reads=[("accA", hb2, c), "anmr"], writes=[("accA", hb2, c)])
                        R.op("act", lambda e, c=c: e.activation(out=yc[:, c, :], in_=accA[:, c, :], func=(AF.Sigmoid if DBG.get("nosilu") else AF.Silu),
                                                                scale=lg[:, c:c + 1], bias=lb[:, c:c + 1]),
                             reads=[("accA", hb2, c)] + par, writes=[("ayc", hb2, c)])
                    for c in range(8):
                        s2 = c % 2
                        R.op("act", lambda e, c=c, s2=s2: e.activation(out=sqt[s2][:], in_=yc[:, c, :], func=AF.Square),
                             reads=[("ayc", hb2, c)], writes=[("asq", s2)])
                        R.op("pe", lambda e, c=c, s2=s2: e.matmul(pS2[:], lhsT=ones_bf[:], rhs=sqt[s2][:], start=(c == 0), stop=(c == 7)),
                             reads=[("asq", s2), "ones_bf"], writes=["aps2"])
                    R.op("act", lambda e: e.activation(out=rs2[:], in_=pS2[:], func=AF.Sqrt, scale=1.0 / 1024, bias=EPS),
                         reads=["aps2"], writes=["ars2"])
                    R.op("dve", lambda e: e.reciprocal(out=rs2[:], in_=rs2[:]), reads=["ars2"], writes=["ars2"])
                    mb = s % 2
                    for c in range(8):
                        eng = "dve"
                        R.op(eng, lambda e, c=c, mb=mb: e.scalar_tensor_tensor(
                            out=mo[mb][:, c, :], in0=yc[:, c, :], scalar=og[:, c:c + 1], in1=rs2[:], op0=ALU.mult, op1=ALU.mult),
                            reads=[("ayc", hb2, c), "ars2"] + par, writes=[("amo", mb, c)])
                    R.dma("pool", f"amo{mb}", lambda e, s=s, mb=mb: e.dma_start(
                        out=mcS.rearrange("c p t -> p c t")[:, :, 256 * s:256 * s + 256], in_=mo[mb][:]),
                        reads=[("amo", mb, c) for c in range(8)], writes=[("mcS", s)])
                    return R.items

                NA_ = DBG.get("p2a_tiles", NT2 // 256)
                for st_ in p2a_conv(0):
                    st_()
                for s in range(NA_):
                    cv = p2a_conv(s + 1) if s + 1 < NA_ else []
                    ch = p2a_chain(s)
                    every = max(1, len(ch) // (len(cv) + 1)) if cv else 10 ** 9
                    ic = 0
                    for i, it in enumerate(ch):
                        if cv and i % every == 0 and ic < len(cv):
                            cv[ic]()
                            ic += 1
                        it()
                    while ic < len(cv):
                        cv[ic]()
                        ic += 1
                    if "p0" in phases:
                        cc_up.emit(3)
                if "p0" in phases:
                    cc_up.flush()
                P.barrier(dram_slots=("cast_win16", "cast_wout16", "cast_wup16", "cast_wdn16"))

        if "p2b" in phases:
            with contextlib.ExitStack() as st:
                qe = [sb(st, f"bqe{i}", [128, 8, 256], BF16) for i in range(2)]
                qo = [sb(st, f"bqo{i}", [128, 8, 256], BF16) for i in range(2)]
                RING = 10
                kb = sb(st, "bk", [128, 8, RING * 128], BF16)
                vb_ = sb(st, "bv", [128, RING, 1040], BF16)
                wres = sb(st, "bwres", [128, 8, 16, 128], BF16)
                bt = sb(st, "bbt", [128, 16, 768], BF16)
                pt = [sb(st, f"bpt{i}", [128, 768], BF16) for i in range(2)]
                yna = [sb(st, f"byna{i}", [128, 1024], F32) for i in range(2)]
                ysq = sb(st, "bysq", [128, 1024], BF16)
                ssq = sb(st, "bssq", [128, 1], F32)
                rden = [sb(st, f"brden{i}", [128, 2], F32) for i in range(2)]
                yn = sb(st, "byn", [128, 1024], BF16)
                gna_s = sb(st, "bgna", [128, 1024], F32)
                mx = [sb(st, f"bmx{i}", [128, 16, 256], BF16) for i in range(2)]
                NWO = 2
                wo = [sb(st, f"bwo{i}", [128, 16, 128], BF16) for i in range(NWO)]
                xc = [sb(st, f"bxc{i}", [128, 256], F32) for i in range(3)]
                x1 = sb(st, "bx1", [128, 16, 256], F32)
                sqb = [sb(st, f"bsqb{i}", [128, 256], BF16) for i in range(2)]
                rm = sb(st, "brm", [128, 256], F32)
                mk = [sb(st, f"bmk{i}", [128, 256], F32) for i in range(2)]
                xn2 = sb(st, "bxn2", [128, 16, 256], BF16)
                gf = sb(st, "bgf", [128, 16], F32)
                psS = ps(st, "bpsS", [128, 2048])
                psO = ps(st, "bpsO", [128, 1024])
                psXX = ps(st, "bpsXX", [128, 1024])
                psX = [psXX[:, 0:512], psXX[:, 512:1024]]
                psT = psXX[:, 512:1024].bitcast(BF16)
                sqacc = sb(st, "bsqacc", [128, 256], F32)
                sqtmp = [sb(st, f"bsqtmp{i}", [128, 256], F32) for i in range(2)]
                sqlo = sb(st, "bsqlo", [128, 256], F32)
                P.dma("sp", "bgna", lambda e: e.dma_start(out=gna_s[:], in_=gna), writes=["bgna"])
                P.dma("sp", "bgf", lambda e: e.dma_start(out=gf[:], in_=g_ffn), writes=["bgf"])
                for i in range(2):
                    P.op("dve", lambda e, i=i: e.memset(qe[i][:], 0.0), writes=[("bqe", i)])
                    P.op("pool", lambda e, i=i: e.memset(qo[i][:], 0.0), writes=[("bqo", i)])
                wcnt = [0]
                cur_tab = [None]
                NS = DBG.get("p2b_tiles", NT2 // 256)
                qS3 = qS.rearrange("c p t -> p c t")
                kS3 = kS.rearrange("c p t -> p c t")
                mcS3 = mcS.rearrange("c p t -> p c t")

                def load_tab(kind):
                    if cur_tab[0] == kind:
                        return
                    cur_tab[0] = kind
                    for hh in range(0, 16, 4):
                        P.dma("pool", "bbt", lambda e, kind=kind, hh=hh: e.dma_start(out=bt[:, hh:hh + 4, :], in_=btab[kind, :, hh:hh + 4, :]),
                              writes=["bbt"])

                def loads(s):
                    a = 256 * s
                    par = s % 2
                    qr = [("qS", j, ti) for j in range(8) for ti in range((a + 384) // 512, (a + 639) // 512 + 1)]
                    P.dma("sp", f"bqe{par}", lambda e: e.dma_start(out=qe[par][0:64, :, :], in_=qS3[0:64, :, a + 384:a + 640]),
                          reads=qr, writes=[("bqe", par)])
                    P.dma("sp", f"bqo{par}", lambda e: e.dma_start(out=qo[par][64:128, :, :], in_=qS3[64:128, :, a + 384:a + 640]),
                          reads=qr, writes=[("bqo", par)])
                    P.dma("sp", f"bmxc{par}", lambda e: e.dma_start(out=mx[par][:, 0:8, :], in_=mcS3[:, :, 256 * s:256 * s + 256]),
                          reads=[("mcS", s)], writes=[("bmx", par, "c")])
                    P.dma("sp", f"bmk{par}", lambda e: e.dma_start(out=mk[par][:], in_=tmask[:, 256 * s:256 * s + 256]), writes=[("bmk", par)])

                def load_kv(g):
                    if g + 1 >= NT1 // 128:
                        return
                    sl = g % RING
                    sp_ = sl // 2
                    ti = (g * 128) // 512
                    P.dma("sp", f"bk{sp_}", lambda e: e.dma_start(out=kb[:, :, sl * 128:(sl + 2) * 128], in_=kS3[:, :, g * 128:(g + 2) * 128]),
                          reads=[("kS", j, ti) for j in range(8)], writes=[("bk", sp_)])
                    P.dma("sp", f"bv{sp_}", lambda e: e.dma_start(out=vb_[:, sl:sl + 2, :], in_=vS[g * 128:(g + 2) * 128, :].rearrange("(c p) f -> p c f", p=128)),
                          reads=[("vS", ti)], writes=[("bv", sp_)])

                def na_steps(s):
                    par = s % 2
                    steps = []

                    def pair(pi):
                        pp = 2 * s - 1 + pi
                        if pp <= 0:
                            kind, nch, c0 = 0, 6, 1 + pi
                        elif pp == 1:
                            kind, nch, c0 = 1, 5, 1 + pi
                        elif pp <= 29:
                            kind, nch, c0 = 2, 5, 1 + pi
                        elif pp == 30:
                            kind, nch, c0 = 3, 5, 1 + pi
                        else:
                            kind, nch, c0 = 4, 6, 0 + pi
                        yb = pi
                        qoff = pi * 128

                        def slot(ci):
                            return (2 * s + c0 + ci) % RING
                        kkeys = sorted(set(("bk", slot(ci) // 2) for ci in range(nch)))
                        vkeys = sorted(set(("bv", slot(ci) // 2) for ci in range(nch)))

                        def s_mm(h):
                            sbuf_i = h % 2
                            hp = h // 2
                            qsrc = qe[par] if h % 2 == 0 else qo[par]
                            base = sbuf_i * 1024

                            def f(e):
                                for ci in range(nch):
                                    e.matmul(psS[:, base + ci * 128: base + (ci + 1) * 128],
                                             lhsT=kb[:, hp, slot(ci) * 128:(slot(ci) + 1) * 128],
                                             rhs=qsrc[:, hp, qoff:qoff + 128], start=True, stop=False)
                                    r = e.matmul(psS[:, base + ci * 128: base + (ci + 1) * 128],
                                                 lhsT=ident[:], rhs=bt[:, h, ci * 128:(ci + 1) * 128], start=False, stop=True)
                                return r
                            P.op("pe", f, reads=kkeys + [("bqe", par), ("bqo", par), "bbt", "ident"], writes=[("psS", sbuf_i)])
                            P.op("act", lambda e: e.activation(out=pt[sbuf_i][:, 0:nch * 128], in_=psS[:, base:base + nch * 128], func=AF.Exp),
                                 reads=[("psS", sbuf_i)], writes=[("bpt", sbuf_i)])

                        def pv_mm(h):
                            sbuf_i = h % 2
                            hp = h // 2
                            ob = hp % 2
                            obase = ob * 512 + (h % 2) * 65

                            def f(e):
                                for ci in range(nch):
                                    r = e.matmul(psO[:, obase:obase + 65], lhsT=pt[sbuf_i][:, ci * 128:(ci + 1) * 128],
                                                 rhs=vb_[:, slot(ci), h * 65:(h + 1) * 65], start=(ci == 0), stop=(ci == nch - 1))
                                return r
                            P.op("pe", f, reads=[("bpt", sbuf_i)] + vkeys, writes=[("psO", ob, 0), ("psO", ob, 1)])
                            if h % 2 == 1:
                                ob0 = ob * 512
                                ov = psO[:, ob0:ob0 + 130].rearrange("p (h d) -> p h d", d=65)
                                P.op("dve", lambda e: e.reciprocal(out=rden[ob][:].rearrange("p (h o) -> p h o", o=1), in_=ov[:, :, 64:65]),
                                     reads=[("psO", ob, 0), ("psO", ob, 1)], writes=[("brden", ob)])
                                for hh in range(2):
                                    P.op("dve", lambda e, hh=hh: e.scalar_tensor_tensor(
                                        out=yna[yb][:, hp * 128 + hh * 64: hp * 128 + hh * 64 + 64],
                                        in0=psO[:, ob0 + hh * 65: ob0 + hh * 65 + 64], scalar=rden[ob][:, hh:hh + 1], in1=ones_f[:, 0:64],
                                        op0=ALU.mult, op1=ALU.mult),
                                        reads=[("psO", ob, hh), ("brden", ob), "ones_f"], writes=[("byna", yb, hp)])

                        def first():
                            load_tab(kind)
                            s_mm(0)
                        steps.append(first)
                        for h in range(16):
                            def hstep(h=h):
                                if h + 1 < 16:
                                    s_mm(h + 1)
                                pv_mm(h)
                            steps.append(hstep)

                        def fin():
                            ynr = [("byna", yb, hp) for hp in range(8)]
                            P.op("act", lambda e: e.activation(out=ysq[:], in_=yna[yb][:], func=AF.Square, accum_out=ssq[:]),
                                 reads=ynr, writes=["bysq", "bssq"])
                            P.op("act", lambda e: e.activation(out=ssq[:], in_=ssq[:], func=AF.Sqrt, scale=1.0 / 1024, bias=EPS),
                                 reads=["bssq"], writes=["bssq"])
                            P.op("dve", lambda e: e.reciprocal(out=ssq[:], in_=ssq[:]), reads=["bssq"], writes=["bssq"])
                            P.op("dve", lambda e: e.scalar_tensor_tensor(out=yn[:], in0=yna[yb][:], scalar=ssq[:, 0:1], in1=gna_s[:],
                                                                         op0=ALU.mult, op1=ALU.mult),
                                 reads=ynr + ["bssq", "bgna"], writes=["byn"])

                            def ft(e):
                                for j in range(8):
                                    r = e.transpose(out=psT[:, j * 128:(j + 1) * 128], in_=yn[:, j * 128:(j + 1) * 128], identity=ident[:])
                                return r
                            P.op("pe", ft, reads=["byn", "ident"], writes=[("bpsX", 1)])
                            P.op("dve", lambda e: e.tensor_copy(
                                out=mx[par][:, 8:16, qoff:qoff + 128], in_=psT.rearrange("p (j t) -> p j t", t=128)),
                                reads=[("bpsX", 1)], writes=[("bmx", par, "n", pi)])
                        steps.append(fin)
                    pair(0)
                    pair(1)
                    return steps

                def outproj_steps(s):
                    a = 256 * s
                    par = s % 2
                    mxr = [("bmx", par, "c"), ("bmx", par, "n", 0), ("bmx", par, "n", 1)]
                    steps = []

                    def dcstep(dc):
                        if dc < 8:
                            wl = wres[:, dc, :, :]
                            wkey = ("bwres", dc)
                        else:
                            w = wcnt[0] % NWO
                            wcnt[0] += 1
                            P.dma("sp", f"bwo{w}", lambda e: e.dma_start(out=wo[w][:], in_=wout16[dc]),
                                  reads=[("wout16", dc)], writes=[("bwo", w)])
                            wl = wo[w]
                            wkey = ("bwo", w)
                        xb = dc % 3
                        P.dma("sp", f"bxc{xb}", lambda e: e.dma_start(out=xc[xb][:], in_=xT3[:, dc, a + 384:a + 640]),
                              writes=[("bxc", xb)])
                        pb = 0

                        def f(e):
                            for j in range(16):
                                r = e.matmul(psX[pb][:, 0:256], lhsT=wl[:, j, :], rhs=mx[par][:, j, :], start=(j == 0), stop=(j == 15))
                            return r
                        P.op("pe", f, reads=[wkey] + mxr, writes=[("bpsX", 0)])
                        P.op("dve", lambda e: e.tensor_tensor(out=x1[:, dc, :], in0=psX[pb][:, 0:256], in1=xc[xb][:], op=ALU.add),
                             reads=[("bpsX", 0), ("bxc", xb)], writes=[("bx1", dc)])
                        s2 = dc % 2
                        P.op("act", lambda e: e.activation(out=(sqacc[:] if dc == 0 else sqtmp[s2][:]), in_=x1[:, dc, :], func=AF.Square),
                             reads=[("bx1", dc)], writes=(["bsqacc"] if dc == 0 else [("bsqtmp", s2)]))
                        if dc > 0:
                            P.op("pool", lambda e: e.tensor_tensor(out=sqacc[:], in0=sqacc[:], in1=sqtmp[s2][:], op=ALU.add),
                                 reads=["bsqacc", ("bsqtmp", s2)], writes=["bsqacc"])
                    for dc in range(16):
                        steps.append(lambda dc=dc: dcstep(dc))
                    return steps

                def finalize(s):
                    par = s % 2
                    P.dma("pool", "bx1o", lambda e: e.dma_start(out=x1S.rearrange("c p t -> p c t")[:, :, 256 * s:256 * s + 256], in_=x1[:]),
                          reads=[("bx1", dc) for dc in range(16)], writes=[("x1S", s)])
                    P.op("act", lambda e: e.activation(out=sqb[0][:], in_=sqacc[:], func=AF.Copy), reads=["bsqacc"], writes=[("bsqb", 0)])
                    P.op("dve", lambda e: e.tensor_tensor(out=sqlo[:], in0=sqacc[:], in1=sqb[0][:], op=ALU.subtract),
                         reads=["bsqacc", ("bsqb", 0)], writes=["bsqlo"])
                    P.op("act", lambda e: e.activation(out=sqb[1][:], in_=sqlo[:], func=AF.Copy), reads=["bsqlo"], writes=[("bsqb", 1)])

                    def fs(e):
                        e.matmul(psX[0][:, 0:256], lhsT=ones_bf[:], rhs=sqb[0][:], start=True, stop=False)
                        return e.matmul(psX[0][:, 0:256], lhsT=ones_bf[:], rhs=sqb[1][:], start=False, stop=True)
                    P.op("pe", fs, reads=[("bsqb", 0), ("bsqb", 1), "ones_bf"], writes=[("bpsX", 0)])
                    P.op("act", lambda e: e.activation(out=rm[:], in_=psX[0][:, 0:256], func=AF.Sqrt, scale=1.0 / D, bias=EPS),
                         reads=[("bpsX", 0)], writes=["brm"])
                    P.op("dve", lambda e: e.reciprocal(out=rm[:], in_=rm[:]), reads=["brm"], writes=["brm"])
                    P.op("dve", lambda e: e.tensor_tensor(out=rm[:], in0=rm[:], in1=mk[par][:], op=ALU.mult), reads=["brm", ("bmk", par)], writes=["brm"])
                    for dc in range(16):
                        P.op("dve", lambda e, dc=dc: e.scalar_tensor_tensor(out=xn2[:, dc, :], in0=x1[:, dc, :], scalar=gf[:, dc:dc + 1], in1=rm[:],
                                                                            op0=ALU.mult, op1=ALU.mult),
                             reads=[("bx1", dc), "brm", "bgf"], writes=[("bxn2", dc)])
                    P.dma("pool", "bxn2o", lambda e: e.dma_start(out=xn2S.rearrange("c p t -> p c t")[:, :, 256 * s:256 * s + 256], in_=xn2[:]),
                          reads=[("bxn2", dc) for dc in range(16)], writes=[("xn2S", s)])

                for g in range(0, 10, 2):
                    load_kv(g)
                loads(0)
                for dc in range(8):
                    if dc >= 2:
                        P.wait("sp", P.bufs[("bwres", dc - 2)][0])
                    P.dma("sp", f"bwres{dc % 2}", lambda e, dc=dc: e.dma_start(out=wres[:, dc, :, :], in_=wout16[dc]),
                          reads=[("wout16", dc)], writes=[("bwres", dc)])
                for stp in na_steps(0):
                    stp()
                for s in range(NS):
                    if s + 2 < NS:
                        load_kv(2 * s + 10)
                    if s + 1 < NS:
                        loads(s + 1)
                        na = na_steps(s + 1)
                    else:
                        na = []
                    op_ = outproj_steps(s)
                    ia = 0
                    for st_ in op_:
                        for _ in range(3):
                            if ia < len(na):
                                na[ia]()
                                ia += 1
                        st_()
                    while ia < len(na):
                        na[ia]()
                        ia += 1
                    finalize(s)
                    if "p0" in phases:
                        cc_dn.emit(1)
                if "p0" in phases:
                    cc_dn.flush()
                P.barrier(dram_slots=("cast_win16", "cast_wout16", "cast_wup16", "cast_wdn16"))

        out_toks = []
        if "p3" in phases:
            with contextlib.ExitStack() as st:
                WM = 458
                xt = [sb(st, f"cxt{i}", [128, 16, WM], BF16) for i in range(2)]
                G = sb(st, "cG", [128, 44, 456], BF16)
                NWU = 3
                wu = [sb(st, f"cwu{i}", [128, 16, 256], BF16) for i in range(NWU)]
                wd = [sb(st, f"cwd{i}", [128, 44, 128], BF16) for i in range(2)]
                cg = [sb(st, f"ccg{i}", [128, 456], F32) for i in range(2)]
                cv = [sb(st, f"ccv{i}", [128, 456], F32) for i in range(2)]
                sg = [sb(st, f"csg{i}", [128, 456], F32) for i in range(2)]
                x2 = sb(st, "cx2", [128, 16, 456], F32)
                x1c = [sb(st, f"cx1c{i}", [128, 456], F32) for i in range(3)]
                sqc = [sb(st, f"csq{i}", [128, 456], BF16) for i in range(2)]
                rf = sb(st, "crf", [128, 456], F32)
                oc = [sb(st, f"coc{i}", [128, 456], F32) for i in range(3)]
                fw = sb(st, "cfw", [128, 88, 3], F32)
                fb = sb(st, "cfb", [128, 88], F32)
                gfin = sb(st, "cgfin", [128, 16], F32)
                pg = [ps(st, f"cpg{i}", [128, 512]) for i in range(2)]
                pv = [ps(st, f"cpv{i}", [128, 512]) for i in range(2)]
                po = [ps(st, f"cpo{i}", [128, 512]) for i in range(2)]
                pq = ps(st, "cpq", [128, 512])
                P.dma("sp", "cfw", lambda e: e.dma_start(out=fw[:], in_=fdw), writes=["cfw"])
                P.dma("sp", "cfb", lambda e: e.dma_start(out=fb[:], in_=fdb), writes=["cfb"])
                P.dma("sp", "cgfin", lambda e: e.dma_start(out=gfin[:], in_=g_fin), writes=["cgfin"])
                par = ["cfw", "cfb"]
                wuc = [0]
                wdc = [0]
                def p3_tile(ti, Wd, t0):
                    WN = Wd + 2
                    a2 = t0 - 1 + OFF2
                    xb = ti % 2
                    P.dma("sp", f"cxt{xb}", lambda e, a2=a2, WN=WN, xb=xb: e.dma_start(
                        out=xt[xb][:, :, 0:WN], in_=xn2S.rearrange("c p t -> p c t")[:, :, a2:a2 + WN]),
                        reads=[("xn2S", s) for s in range(a2 // 256, (a2 + WN - 1) // 256 + 1)], writes=[("cxt", xb)])
                    def p3_up(j):
                        w = wuc[0] % NWU
                        wuc[0] += 1
                        P.dma("sp", f"cwu{w}", lambda e, j=j, w=w: e.dma_start(out=wu[w][:], in_=wup16[j]),
                              reads=[("wup16", j)], writes=[("cwu", w)])
                        pb = j % 2
                        for half, (pt_, pkey) in enumerate(((pg[pb], ("cpg", pb)), (pv[pb], ("cpv", pb)))):
                            def f(e, w=w, pt_=pt_, half=half, xb=xb, WN=WN):
                                for c in range(16):
                                    r = e.matmul(pt_[:, 0:WN], lhsT=wu[w][:, c, half * 128:(half + 1) * 128], rhs=xt[xb][:, c, 0:WN],
                                                 start=(c == 0), stop=(c == 15))
                                return r
                            P.op("pe", f, reads=[("cwu", w), ("cxt", xb)], writes=[pkey])
                        for half, (pt_, pkey, dst, dkey) in enumerate(((pg[pb], ("cpg", pb), cg[pb], ("ccg", pb)),
                                                                         (pv[pb], ("cpv", pb), cv[pb], ("ccv", pb)))):
                            ch = j + 44 * half
                            P.op("act", lambda e, pt_=pt_, dst=dst, ch=ch, Wd=Wd: e.activation(
                                out=dst[:, 0:Wd], in_=pt_[:, 2:Wd + 2], func=AF.Identity, scale=fw[:, ch, 2:3], bias=fb[:, ch:ch + 1]),
                                reads=[pkey] + par, writes=[dkey])
                            P.op("dve", lambda e, pt_=pt_, dst=dst, ch=ch, Wd=Wd: e.scalar_tensor_tensor(
                                out=dst[:, 0:Wd], in0=pt_[:, 1:Wd + 1], scalar=fw[:, ch, 1:2], in1=dst[:, 0:Wd], op0=ALU.mult, op1=ALU.add),
                                reads=[pkey, dkey] + par, writes=[dkey])
                        for half, (pt_, pkey, dst, dkey) in enumerate(((pg[pb], ("cpg", pb), cg[pb], ("ccg", pb)),
                                                                         (pv[pb], ("cpv", pb), cv[pb], ("ccv", pb)))):
                            ch = j + 44 * half
                            P.op("dve", lambda e, pt_=pt_, dst=dst, ch=ch, Wd=Wd: e.scalar_tensor_tensor(
                                out=dst[:, 0:Wd], in0=pt_[:, 0:Wd], scalar=fw[:, ch, 0:1], in1=dst[:, 0:Wd], op0=ALU.mult, op1=ALU.add),
                                reads=[pkey, dkey] + par, writes=[dkey])
                        P.op("act", lambda e, pb=pb, Wd=Wd: e.activation(out=sg[pb][:, 0:Wd], in_=cg[pb][:, 0:Wd], func=AF.Silu),
                             reads=[("ccg", pb)], writes=[("csg", pb)])
                        P.op("pool", lambda e, pb=pb, j=j, Wd=Wd: e.tensor_tensor(out=G[:, j, 0:Wd], in0=sg[pb][:, 0:Wd], in1=cv[pb][:, 0:Wd], op=ALU.mult),
                             reads=[("csg", pb), ("ccv", pb)], writes=[("cG", j)])
                    for j in range(44):
                        p3_up(j)
                    Gr = [("cG", j) for j in range(44)]
                    def p3_dn(dc):
                        w = wdc[0] % 2
                        wdc[0] += 1
                        P.dma("sp", f"cwd{w}", lambda e, dc=dc, w=w: e.dma_start(out=wd[w][:], in_=wdn16[dc]),
                              reads=[("wdn16", dc)], writes=[("cwd", w)])
                        xb3 = dc % 3
                        P.dma("sp", f"cx1c{xb3}", lambda e, dc=dc, xb3=xb3, t0=t0, Wd=Wd: e.dma_start(
                            out=x1c[xb3][:, 0:Wd], in_=x1S[dc, :, t0 + OFF2:t0 + OFF2 + Wd]),
                            reads=[("x1S", s) for s in range((t0 + OFF2) // 256, (t0 + OFF2 + Wd - 1) // 256 + 1)], writes=[("cx1c", xb3)])
                        pb = dc % 2

                        def f(e, w=w, pb=pb, Wd=Wd):
                            for j in range(44):
                                r = e.matmul(po[pb][:, 0:Wd], lhsT=wd[w][:, j, :], rhs=G[:, j, 0:Wd], start=(j == 0), stop=(j == 43))
                            return r
                        P.op("pe", f, reads=[("cwd", w)] + Gr, writes=[("cpo", pb)])
                        P.op("dve", lambda e, dc=dc, pb=pb, xb3=xb3, Wd=Wd: e.tensor_tensor(
                            out=x2[:, dc, 0:Wd], in0=po[pb][:, 0:Wd], in1=x1c[xb3][:, 0:Wd], op=ALU.add),
                            reads=[("cpo", pb), ("cx1c", xb3)], writes=[("cx2", dc)])
                        s2 = dc % 2
                        P.op("act", lambda e, dc=dc, s2=s2, Wd=Wd: e.activation(out=sqc[s2][:, 0:Wd], in_=x2[:, dc, 0:Wd], func=AF.Square),
                             reads=[("cx2", dc)], writes=[("csq", s2)])
                        P.op("pe", lambda e, dc=dc, s2=s2, Wd=Wd: e.matmul(pq[:, 0:Wd], lhsT=ones_bf[:], rhs=sqc[s2][:, 0:Wd], start=(dc == 0), stop=(dc == 15)),
                             reads=[("csq", s2), "ones_bf"], writes=["cpq"])
                    for dc in range(16):
                        p3_dn(dc)
                    P.op("act", lambda e, Wd=Wd: e.activation(out=rf[:, 0:Wd], in_=pq[:, 0:Wd], func=AF.Sqrt, scale=1.0 / D, bias=EPS),
                         reads=["cpq"], writes=["crf"])
                    P.op("dve", lambda e, Wd=Wd: e.reciprocal(out=rf[:, 0:Wd], in_=rf[:, 0:Wd]), reads=["crf"], writes=["crf"])
                    for dc in range(16):
                        ob = dc % 3
                        eng = "dve"
                        P.op(eng, lambda e, dc=dc, ob=ob, Wd=Wd: e.scalar_tensor_tensor(
                            out=oc[ob][:, 0:Wd], in0=x2[:, dc, 0:Wd], scalar=gfin[:, dc:dc + 1], in1=rf[:, 0:Wd], op0=ALU.mult, op1=ALU.mult),
                            reads=[("cx2", dc), "crf", "cgfin"], writes=[("coc", ob)])
                        tk = P.dma("sp", f"coc{ob}", lambda e, dc=dc, ob=ob, t0=t0, Wd=Wd: e.dma_start(
                            out=yT[dc * 128:(dc + 1) * 128, t0:t0 + Wd], in_=oc[ob][:, 0:Wd]),
                            reads=[("coc", ob)], writes=[("yT", dc, ti)])
                        out_toks.append(tk)
                t0 = 0
                for ti, Wd in enumerate(P3W[:DBG.get("p3_tiles", 9)]):
                    p3_tile(ti, Wd, t0)
                    t0 += Wd
                P.barrier(dram_slots=("cast_win16", "cast_wout16", "cast_wup16", "cast_wdn16"))
        P.barrier(dram_slots=("cast_win16", "cast_wout16", "cast_wup16", "cast_wdn16"))
        for s in P.slots.values():
            if s[1] > 0:
                P.wait("sp", (s[0], s[1]))
        if DBG.get("verbose"):
            print("nsem", P.nsem, {e: P.cnt[e] for e in ENGS})
        P.replay()
    return nc


def _fm(v, nch):
    return np.ascontiguousarray(v.reshape(nch, 128).T)


def _bias_tables(rpb, core):
    q = core % 4
    R0 = q * 64
    rows = SEQ // 64
    out = np.full((5, 128, 16, 768), NEG, np.float32)
    kinds = [(0, -4, 6), (1, -4, 5), (2, -4, 5), (30, -4, 5), (31, -6, 6)]
    kp = np.arange(128)
    k_par, k_col = kp // 64, kp % 64
    qp = np.arange(128)
    q_par, q_col = qp // 64, qp % 64
    cs = np.clip(q_col - 8, 0, 64 - 16)
    for ki, (pp, rel, nch) in enumerate(kinds):
        for ci in range(nch):
            rq = R0 + 2 * pp + q_par
            rk = R0 + 2 * pp + rel + 2 * ci + k_par
            fake = (rq < 0) | (rq >= rows)
            rs = np.where(fake, rq - 4, np.clip(rq - 4, 0, rows - 8))
            dr = rk[:, None] - rq[None, :] + 7
            okr = (rk[:, None] >= rs[None, :]) & (rk[:, None] < rs[None, :] + 8)
            okr &= (fake[None, :] | ((rk[:, None] >= 0) & (rk[:, None] < rows)))
            dc = k_col[:, None] - q_col[None, :] + 15
            okc = (k_col[:, None] >= cs[None, :]) & (k_col[:, None] < cs[None, :] + 16)
            ok = okr & okc
            drc = np.clip(dr, 0, 14)
            dcc = np.clip(dc, 0, 30)
            vals = rpb[:, drc, dcc]
            blk = np.where(ok[None], vals, np.float32(NEG)).astype(np.float32)
            out[ki, :, :, ci * 128:(ci + 1) * 128] = blk.transpose(1, 0, 2)
    return out


def prep_inputs(x, attn_norm_g, w_in, conv_dw_w, conv_dw_b, conv_ln_g, conv_ln_b, rpb, conv_out_g, na_out_g,
                w_out, ffn_norm_g, w_up, ffn_dw_w, ffn_dw_b, w_down, final_norm_g):
    f = np.float32
    x = np.asarray(x, f)
    w_in = np.asarray(w_in, f)[0]
    w_out = np.asarray(w_out, f)[0]
    w_up = np.asarray(w_up, f)[0]
    w_down = np.asarray(w_down, f)[0]
    win = np.ascontiguousarray(w_in.reshape(16, 128, 40, 128).transpose(2, 1, 0, 3))
    wout = np.ascontiguousarray(w_out.reshape(16, 128, 16, 128).transpose(2, 1, 0, 3))
    wg = w_up[:, :DFF].reshape(16, 128, 44, 128).transpose(2, 1, 0, 3)
    wv = w_up[:, DFF:].reshape(16, 128, 44, 128).transpose(2, 1, 0, 3)
    wup = np.ascontiguousarray(np.concatenate([wg, wv], axis=3))
    wdn = np.ascontiguousarray(w_down.reshape(44, 128, 16, 128).transpose(2, 1, 0, 3))
    shared = {
        "win": win, "wout": wout, "wup": wup, "wdn": wdn,
        "g_attn": _fm(np.asarray(attn_norm_g, f)[0], 16),
        "cdw": np.ascontiguousarray(np.asarray(conv_dw_w, f)[0].T.reshape(8, 128, 31).transpose(1, 0, 2)),
        "cdb": _fm(np.asarray(conv_dw_b, f)[0], 8),
        "clg": _fm(np.asarray(conv_ln_g, f)[0], 8),
        "clb": _fm(np.asarray(conv_ln_b, f)[0], 8),
        "cog": _fm(np.asarray(conv_out_g, f)[0], 8),
        "gna": np.ascontiguousarray(np.broadcast_to(np.asarray(na_out_g, f)[0][None, :], (128, 1024))),
        "g_ffn": _fm(np.asarray(ffn_norm_g, f)[0], 16),
        "fdw": np.ascontiguousarray(np.asarray(ffn_dw_w, f)[0].T.reshape(88, 128, 3).transpose(1, 0, 2)),
        "fdb": _fm(np.asarray(ffn_dw_b, f)[0], 88),
        "g_fin": _fm(np.asarray(final_norm_g, f), 16),
        "ident_in": np.eye(128, dtype=f),
    }
    rp = np.asarray(rpb, f)[0]
    maps = []
    for c in range(NCORE):
        b, q = c // 4, c % 4
        T0 = q * TPC
        lo, hi = T0 - OFF1, T0 - OFF1 + NT1
        xt = np.zeros((D, NT1), f)
        slo, shi = max(lo, 0), min(hi, SEQ)
        xt[:, slo - lo:shi - lo] = x[b, slo:shi, :].T
        tm = np.zeros((NT2,), f)
        lo2 = T0 - OFF2
        s2lo, s2hi = max(lo2, 0), min(lo2 + NT2, SEQ)
        tm[s2lo - lo2:s2hi - lo2] = 1.0
        m = dict(shared)
        m["xT"] = xt
        m["btab"] = _bias_tables(rp, c)
        m["tmask"] = np.ascontiguousarray(np.broadcast_to(tm[None, :], (128, NT2)))
        maps.append(m)
    return maps


def kernel(**inputs):
    maps = prep_inputs(**inputs)
    nc = build()
    res = run_bass_kernel_spmd(nc, maps, core_ids=list(range(NCORE)))
    out = np.empty((2, SEQ, D), np.float32)
    for c in range(NCORE):
        b, q = c // 4, c % 4
        out[b, q * TPC:(q + 1) * TPC, :] = res.results[c]["yT"].T
    return out
```

```python
import contextlib
import os
import numpy as np
import concourse.bass as bass
import concourse.mybir as mybir
from concourse.bass_utils import run_bass_kernel_spmd

F32 = mybir.dt.float32
BF16 = mybir.dt.bfloat16
AF = mybir.ActivationFunctionType
ALU = mybir.AluOpType

ENGS = ("pe", "act", "dve", "pool", "sp")
EPOCH = 30000

D = 2048
SEQ = 16384
NCORE = 8
TPC = 4096
NT1 = 5120
OFF1 = 512
NT2 = 4352
OFF2 = 128
DFF = 5632
EPS = 1e-6
NEG = -30000.0
P3W = [456] * 8 + [448]
DBG = {}


class Prog:
    def __init__(self, nc, stack):
        self.nc = nc
        self.stack = stack
        self.q = {e: [] for e in ENGS}
        self.cnt = {e: 0 for e in ENGS}
        self.csem = {}
        self.waited = {e: {} for e in ENGS}
        self.bufs = {}
        self.slots = {}
        self.nsem = 0
        for e in ("pe", "act", "dve", "pool"):
            self._new_csem(e)

    def sem(self, name):
        self.nsem += 1
        return self.stack.enter_context(self.nc.semaphore(f"{name}_{self.nsem}"))

    def _new_csem(self, e):
        self.csem[e] = self.sem("c" + e)
        self.cnt[e] = 0

    def _wait(self, e, tok):
        s, v = tok
        w = self.waited[e]
        if w.get(id(s), 0) >= v:
            return
        w[id(s)] = v
        self.q[e].append(("wait", s, v))

    def _deps(self, e, reads, writes):
        for k in reads:
            b = self.bufs.get(k)
            if b and b[0] is not None:
                self._wait(e, b[0])
        for k in writes:
            b = self.bufs.get(k)
            if b:
                if b[0] is not None:
                    self._wait(e, b[0])
                for t in b[1].values():
                    self._wait(e, t)

    def _mark(self, tok, reads, writes):
        for k in reads:
            b = self.bufs.setdefault(k, [None, {}])
            o = b[1].get(id(tok[0]))
            if o is None or o[1] < tok[1]:
                b[1][id(tok[0])] = tok
        for k in writes:
            self.bufs[k] = [tok, {}]

    def op(self, e, fn, reads=(), writes=()):
        if self.cnt[e] >= EPOCH:
            self._new_csem(e)
        self._deps(e, reads, writes)
        self.cnt[e] += 1
        tok = (self.csem[e], self.cnt[e])
        self.q[e].append(("op", fn, self.csem[e]))
        self._mark(tok, reads, writes)
        return tok

    def dma(self, e, slot, fn, reads=(), writes=()):
        if slot not in self.slots:
            self.slots[slot] = [self.sem("d"), 0]
        sl = self.slots[slot]
        if sl[1] >= EPOCH:
            sl[0] = self.sem("d")
            sl[1] = 0
        self._deps(e, reads, writes)
        sl[1] += 16
        tok = (sl[0], sl[1])
        self.q[e].append(("dma", fn, sl[0]))
        self._mark(tok, reads, writes)
        return tok

    def wait(self, e, tok):
        self._wait(e, tok)

    def barrier(self, dram_slots=()):
        toks = [(self.csem[e], self.cnt[e]) for e in ("pe", "act", "dve", "pool") if self.cnt[e] > 0]
        toks += [(s[0], s[1]) for k, s in self.slots.items() if s[1] > 0 and k not in dram_slots]
        for e in ENGS:
            for t in toks:
                self._wait(e, t)

    def replay(self):
        engobj = {"pe": "tensor", "act": "scalar", "dve": "vector", "pool": "gpsimd", "sp": "sync"}
        with self.nc.Block() as block:
            for e in ENGS:
                items = self.q[e]
                if not items:
                    continue

                def body(eng, items=items):
                    for it in items:
                        if it[0] == "wait":
                            eng.wait_ge(it[1], it[2])
                        elif it[0] == "op":
                            it[1](eng).then_inc(it[2], 1)
                        else:
                            it[1](eng).then_inc(it[2], 16)

                getattr(block, engobj[e])(body)


def build(phases=("p0", "p1", "p2a", "p2b", "p3"), debug=False):
    nc = bass.Bass("TRN2", target_bir_lowering=False)
    dbgset = set(debug) if debug else set()

    def din(name, shape, dt=F32):
        return nc.dram_tensor(name, shape, dt, kind="ExternalInput").ap()

    PROD = {"win16": "p0", "wout16": "p0", "wup16": "p0", "wdn16": "p0", "hS": "p1", "qS": "p1", "kS": "p1", "vS": "p1",
            "mcS": "p2a", "x1S": "p2b", "xn2S": "p2b"}

    def dscr(name, shape, dt, force_in=False):
        if PROD[name] not in phases:
            return nc.dram_tensor(name, shape, dt, kind="ExternalInput").ap()
        if name in dbgset:
            return nc.dram_tensor(name, shape, dt, kind="ExternalOutput").ap()
        return nc.dram_tensor(name, shape, dt).ap()

    xT = din("xT", [D, NT1])
    win = din("win", [40, 128, 16, 128])
    wout = din("wout", [16, 128, 16, 128])
    wup = din("wup", [44, 128, 16, 256])
    wdn = din("wdn", [16, 128, 44, 128])
    g_attn = din("g_attn", [128, 16])
    cdw = din("cdw", [128, 8, 31])
    cdb = din("cdb", [128, 8])
    clg = din("clg", [128, 8])
    clb = din("clb", [128, 8])
    cog = din("cog", [128, 8])
    gna = din("gna", [128, 1024])
    g_ffn = din("g_ffn", [128, 16])
    fdw = din("fdw", [128, 88, 3])
    fdb = din("fdb", [128, 88])
    g_fin = din("g_fin", [128, 16])
    btab = din("btab", [5, 128, 16, 768])
    tmask = din("tmask", [128, NT2])
    ident_in = din("ident_in", [128, 128])
    yT = nc.dram_tensor("yT", [D, TPC], F32, kind="ExternalOutput").ap()

    win16 = dscr("win16", [40, 128, 16, 128], BF16)
    wout16 = dscr("wout16", [16, 128, 16, 128], BF16)
    wup16 = dscr("wup16", [44, 128, 16, 256], BF16)
    wdn16 = dscr("wdn16", [16, 128, 44, 128], BF16)
    hS = dscr("hS", [8, 128, NT1], BF16)
    qS = dscr("qS", [8, 128, NT1], BF16)
    kS = dscr("kS", [8, 128, NT1], BF16)
    vS = dscr("vS", [NT1, 1040], BF16)
    mcS = dscr("mcS", [8, 128, NT2], BF16)
    x1S = dscr("x1S", [16, 128, NT2], F32)
    xn2S = dscr("xn2S", [16, 128, NT2], BF16)

    xT3 = xT.rearrange("(c p) t -> p c t", p=128)

    with contextlib.ExitStack() as top:
        P = Prog(nc, top)

        def sb(st, name, shape, dt):
            return st.enter_context(nc.sbuf_tensor(name, shape, dt))

        def ps(st, name, shape, dt=F32):
            return st.enter_context(nc.psum_tensor(name, shape, dt))

        ones_bf = sb(top, "ones_bf", [128, 128], BF16)
        ones_f = sb(top, "ones_f", [128, 128], F32)
        ident = sb(top, "ident", [128, 128], BF16)
        identf = sb(top, "identf", [128, 128], F32)
        P.op("dve", lambda e: e.memset(ones_bf[:], 1.0), writes=["ones_bf"])
        P.op("dve", lambda e: e.memset(ones_f[:], 1.0), writes=["ones_f"])
        P.dma("sp", "identf", lambda e: e.dma_start(out=identf[:], in_=ident_in), writes=["identf"])
        P.op("dve", lambda e: e.tensor_copy(out=ident[:], in_=identf[:]), reads=["identf"], writes=["ident"])

        if "p0" in phases:
            def cast(dst, src, rows, step, key):
                d2 = dst.rearrange("a p k m -> (a p) (k m)")
                s2 = src.rearrange("a p k m -> (a p) (k m)")
                tok = None
                for r in range(0, rows, step):
                    tok = P.dma("pool", "cast_" + key, lambda e, r=r: e.dma_start(out=d2[r:r + step, :], in_=s2[r:r + step, :]))
                for a in range(rows // 128):
                    P.bufs[(key, a)] = [tok, {}]
            cast(win16, win, 40 * 128, 512, "win16")
            class ChunkCast:
                def __init__(self, dst, src, n, key):
                    self.d2 = dst.rearrange("a p k m -> (a p) (k m)")
                    self.s2 = src.rearrange("a p k m -> (a p) (k m)")
                    self.n, self.key, self.i, self.tok = n, key, 0, None

                def emit(self, cnt=1):
                    for _ in range(cnt):
                        if self.i >= self.n:
                            return
                        r = self.i * 128
                        self.tok = P.dma("pool", "cast_" + self.key, lambda e, r=r: e.dma_start(out=self.d2[r:r + 128, :], in_=self.s2[r:r + 128, :]))
                        self.i += 1
                        if self.i == self.n:
                            for a in range(self.n):
                                P.bufs[(self.key, a)] = [self.tok, {}]

                def flush(self):
                    self.emit(self.n)
            cc_up = ChunkCast(wup16, wup, 44, "wup16")
            cc_dn = ChunkCast(wdn16, wdn, 16, "wdn16")
            late_casts = {2: lambda: cast(wout16, wout, 16 * 128, 512, "wout16")}
            if not all(p in phases for p in ("p1", "p2a", "p2b")):
                for k in list(late_casts):
                    late_casts[k]()
                late_casts = {}
                cc_up.flush()
                cc_dn.flush()

        if "p1" in phases:
            with contextlib.ExitStack() as st:
                xs = [sb(st, f"p1x{i}", [128, 16, 512], F32) for i in range(2)]
                xn = [sb(st, f"p1xn{i}", [128, 16, 512], BF16) for i in range(2)]
                sq = [sb(st, f"p1sq{i}", [128, 512], BF16) for i in range(2)]
                NWB = 5
                wb = [sb(st, f"p1w{i}", [128, 16, 128], BF16) for i in range(NWB)]
                wv = [sb(st, f"p1wv{i}", [128, 4, 16, 128], BF16) for i in range(2)]
                wst = [sb(st, f"p1wst{i}", [128, 16, 128], F32) for i in range(4)]
                NDIRECT = DBG.get("p1_direct", 1)
                wsc = [0]
                rstd = sb(st, "p1rstd", [128, 512], F32)
                gat = sb(st, "p1g", [128, 16], F32)
                sig = [sb(st, f"p1sig{i}", [128, 512], F32) for i in range(2)]
                hst = [sb(st, f"p1h{i}", [128, 512], BF16) for i in range(2)]
                qst = [sb(st, f"p1q{i}", [128, 512], BF16) for i in range(3)]
                vst = [sb(st, f"p1v{i}", [128, 16, 65], BF16) for i in range(2)]
                pA = [ps(st, f"p1pA{i}", [128, 512]) for i in range(2)]
                pB = [ps(st, f"p1pB{i}", [128, 512]) for i in range(2)]
                pS = ps(st, "p1pS", [128, 512])
                P.dma("sp", "p1g", lambda e: e.dma_start(out=gat[:], in_=g_attn), writes=["p1g"])
                for i in range(2):
                    P.op("pool", lambda e, i=i: e.memset(vst[i][:], 1.0), writes=[("vst", i)])
                wcnt = [0]
                NTILE = NT1 // 512
                def p1_norm(ti):
                    a = ti * 512
                    b2 = ti % 2
                    P.dma("sp", f"p1x{b2}", lambda e, a=a, b2=b2: e.dma_start(out=xs[b2][:], in_=xT3[:, :, a:a + 512]),
                          writes=[("xs", b2)])
                    for c in range(16):
                        s2 = c % 2
                        P.op("act", lambda e, c=c, s2=s2, b2=b2: e.activation(out=sq[s2][:], in_=xs[b2][:, c, :], func=AF.Square),
                             reads=[("xs", b2)], writes=[("sq", s2)])
                        P.op("pe", lambda e, c=c, s2=s2: e.matmul(pS[:], lhsT=ones_bf[:], rhs=sq[s2][:], start=(c == 0), stop=(c == 15)),
                             reads=[("sq", s2), "ones_bf"], writes=["pS"])
                    P.op("act", lambda e: e.activation(out=rstd[:], in_=pS[:], func=AF.Sqrt, scale=1.0 / D, bias=EPS),
                         reads=["pS"], writes=["rstd"])
                    P.op("dve", lambda e: e.reciprocal(out=rstd[:], in_=rstd[:]), reads=["rstd"], writes=["rstd"])
                    for c in range(16):
                        eng = "dve"
                        P.op(eng, lambda e, c=c, b2=b2: e.scalar_tensor_tensor(
                            out=xn[b2][:, c, :], in0=xs[b2][:, c, :], scalar=gat[:, c:c + 1], in1=rstd[:],
                            op0=ALU.mult, op1=ALU.mult),
                            reads=[("xs", b2), "rstd", "p1g"], writes=[("xn", b2, c)])

                def p1_tile(ti):
                    a = ti * 512
                    b2 = ti % 2
                    xnr = [("xn", b2, c) for c in range(16)]

                    def load_w(cc):
                        w = wcnt[0] % NWB
                        wcnt[0] += 1
                        if ti < NDIRECT:
                            k = wsc[0] % 4
                            wsc[0] += 1
                            P.dma("sp", f"p1wst{k}", lambda e, cc=cc, k=k: e.dma_start(out=wst[k][:], in_=win[cc]), writes=[("wst", k)])
                            eng = "pool" if wsc[0] % 4 == 0 else "dve"
                            P.op(eng, lambda e, k=k, w=w: e.tensor_copy(out=wb[w][:], in_=wst[k][:]), reads=[("wst", k)], writes=[("wb", w)])
                            return w
                        P.dma("sp", f"p1w{w}", lambda e, cc=cc, w=w: e.dma_start(out=wb[w][:], in_=win16[cc]),
                              reads=[("win16", cc)], writes=[("wb", w)])
                        return w

                    def mm_fm(psum, pkey, w):
                        def f(e):
                            for c in range(16):
                                r = e.matmul(psum[:], lhsT=wb[w][:, c, :], rhs=xn[b2][:, c, :], start=(c == 0), stop=(c == 15))
                            return r
                        P.op("pe", f, reads=[("wb", w)] + xnr, writes=[pkey])

                    def p1_u(j):
                        pb = j % 2
                        w1 = load_w(j)
                        w2 = load_w(8 + j)
                        mm_fm(pA[pb], ("pA", pb), w1)
                        mm_fm(pB[pb], ("pB", pb), w2)
                        P.op("act", lambda e, pb=pb: e.activation(out=sig[pb][:], in_=pB[pb][:], func=AF.Sigmoid),
                             reads=[("pB", pb)], writes=[("sig", pb)])
                        P.op("dve", lambda e, pb=pb: e.tensor_tensor(out=hst[pb][:], in0=pA[pb][:], in1=sig[pb][:], op=ALU.mult),
                             reads=[("pA", pb), ("sig", pb)], writes=[("hst", pb)])
                        P.dma("pool", f"p1h{pb}", lambda e, j=j, pb=pb, a=a: e.dma_start(out=hS[j, :, a:a + 512], in_=hst[pb][:]),
                              reads=[("hst", pb)], writes=[("hS", j, ti)])
                    for j in range(8):
                        p1_u(j)
                    if ti + 1 < DBG.get("p1_tiles", NTILE):
                        p1_norm(ti + 1)
                    def p1_qk(j):
                        pb = j % 2
                        w1 = load_w(16 + j)
                        mm_fm(pA[pb], ("pA", pb), w1)
                        qb = j % 3
                        if j < 8:
                            P.op("act", lambda e, pb=pb, qb=qb: e.activation(out=qst[qb][:], in_=pA[pb][:], func=AF.Copy, scale=0.125),
                                 reads=[("pA", pb)], writes=[("qst", qb)])
                            P.dma("pool", f"p1q{qb}", lambda e, j=j, qb=qb, a=a: e.dma_start(out=qS[j, :, a:a + 512], in_=qst[qb][:]),
                                  reads=[("qst", qb)], writes=[("qS", j, ti)])
                        else:
                            P.op("dve", lambda e, pb=pb, qb=qb: e.tensor_copy(out=qst[qb][:], in_=pA[pb][:]),
                                 reads=[("pA", pb)], writes=[("qst", qb)])
                            P.dma("pool", f"p1q{qb}", lambda e, j=j, qb=qb, a=a: e.dma_start(out=kS[j - 8, :, a:a + 512], in_=qst[qb][:]),
                                  reads=[("qst", qb)], writes=[("kS", j - 8, ti)])
                    for j in range(16):
                        p1_qk(j)
                    for hb in range(2):
                        if ti < NDIRECT:
                            for a4 in range(4):
                                k = wsc[0] % 4
                                wsc[0] += 1
                                P.dma("sp", f"p1wst{k}", lambda e, hb=hb, a4=a4, k=k: e.dma_start(out=wst[k][:], in_=win[32 + 4 * hb + a4]), writes=[("wst", k)])
                                eng = "pool" if wsc[0] % 4 == 0 else "dve"
                                P.op(eng, lambda e, hb=hb, a4=a4, k=k: e.tensor_copy(out=wv[hb][:, a4, :, :], in_=wst[k][:]),
                                     reads=[("wst", k)], writes=[("wv", hb)])
                            continue
                        P.dma("sp", f"p1wv{hb}", lambda e, hb=hb: e.dma_start(
                            out=wv[hb][:], in_=win16[32 + 4 * hb:36 + 4 * hb].rearrange("a p k m -> p a k m")),
                            reads=[("win16", 32 + 4 * hb + i) for i in range(4)], writes=[("wv", hb)])
                    def p1_v(tsub):
                      vb = tsub % 2
                      for hb in range(2):
                            pb = (tsub * 2 + hb) % 2

                            def f(e, hb=hb, tsub=tsub, pb=pb):
                                for c in range(16):
                                    r = e.matmul(pB[pb][:], lhsT=xn[b2][:, c, tsub * 128:(tsub + 1) * 128],
                                                 rhs=wv[hb][:, :, c, :], start=(c == 0), stop=(c == 15))
                                return r
                            P.op("pe", f, reads=[("wv", hb)] + xnr, writes=[("pB", pb)])
                            eng = "act" if hb == 0 else "dve"
                            if eng == "act":
                                P.op("act", lambda e, hb=hb, pb=pb, vb=vb: e.activation(
                                    out=vst[vb][:, hb * 8:(hb + 1) * 8, 0:64],
                                    in_=pB[pb][:].rearrange("p (h d) -> p h d", d=64), func=AF.Copy),
                                    reads=[("pB", pb)], writes=[("vst", vb)])
                            else:
                                P.op("dve", lambda e, hb=hb, pb=pb, vb=vb: e.tensor_copy(
                                    out=vst[vb][:, hb * 8:(hb + 1) * 8, 0:64],
                                    in_=pB[pb][:].rearrange("p (h d) -> p h d", d=64)),
                                    reads=[("pB", pb)], writes=[("vst", vb)])
                      P.dma("pool", f"p1v{vb}", lambda e, tsub=tsub, vb=vb, a=a: e.dma_start(
                            out=vS[a + tsub * 128:a + (tsub + 1) * 128, :], in_=vst[vb][:].rearrange("p h d -> p (h d)")),
                            reads=[("vst", vb)], writes=[("vS", ti)])
                    for tsub in range(4):
                        p1_v(tsub)
                p1_norm(0)
                for ti in range(DBG.get("p1_tiles", NTILE)):
                    p1_tile(ti)
                    if "p0" in phases and ti in late_casts:
                        late_casts.pop(ti)()
                if "p0" in phases and 2 in late_casts:
                    late_casts.pop(2)()
                P.barrier(dram_slots=("cast_win16", "cast_wout16", "cast_wup16", "cast_wdn16"))

        if "p2a" in phases:
            with contextlib.ExitStack() as st:
                hb_ = [sb(st, f"ah{i}", [128, 8, 286], BF16) for i in range(2)]
                accA2 = [sb(st, f"aaccA{i}", [128, 8, 256], F32) for i in range(2)]
                diagw = sb(st, "adiag", [128, 8, 31, 128], BF16)
                pcv = [ps(st, f"apcv{i}", [128, 512]) for i in range(4)]
                sqt = [sb(st, f"asq{i}", [128, 256], BF16) for i in range(2)]
                mean = sb(st, "amean", [128, 256], F32)
                msq = sb(st, "amsq", [128, 256], F32)
                rs = sb(st, "ars", [128, 256], F32)
                nmr = sb(st, "anmr", [128, 256], F32)
                rs2 = sb(st, "ars2", [128, 256], F32)
                yc2 = [sb(st, f"ayc{i}", [128, 8, 256], F32) for i in range(2)]
                mo = [sb(st, f"amo{i}", [128, 8, 256], BF16) for i in range(2)]
                w_dw = sb(st, "awdw", [128, 8, 31], F32)
                b_dw = sb(st, "abdw", [128, 8], F32)
                lg = sb(st, "alg", [128, 8], F32)
                lb = sb(st, "alb", [128, 8], F32)
                og = sb(st, "aog", [128, 8], F32)
                pSt = ps(st, "apst", [128, 512])
                pS2 = ps(st, "aps2", [128, 256])
                for t_, src, key in ((w_dw, cdw, "awdw"), (b_dw, cdb, "abdw"), (lg, clg, "alg"), (lb, clb, "alb"), (og, cog, "aog")):
                    P.dma("sp", key, lambda e, t_=t_, src=src: e.dma_start(out=t_[:], in_=src), writes=[key])
                par = ["awdw", "abdw", "alg", "alb", "aog"]
                for c in range(8):
                    for k in range(31):
                        if k % 2 == 0:
                            P.op("act", lambda e, c=c, k=k: e.activation(out=diagw[:, c, k, :], in_=identf[:], func=AF.Copy, scale=w_dw[:, c, k:k + 1]),
                                 reads=["identf", "awdw"], writes=[("adiag", c, 0)])
                        else:
                            P.op("dve", lambda e, c=c, k=k: e.scalar_tensor_tensor(out=diagw[:, c, k, :], in0=identf[:], scalar=w_dw[:, c, k:k + 1], in1=identf[:],
                                                                                   op0=ALU.mult, op1=ALU.mult),
                                 reads=["identf", "awdw"], writes=[("adiag", c, 1)])
                def p2a_conv(s):
                    steps = []
                    hb2 = s % 2
                    accA = accA2[hb2]
                    yc = yc2[hb2]
                    a1 = 256 * s + 384 - 15
                    steps.append(lambda: P.dma("sp", f"ah{hb2}", lambda e: e.dma_start(
                        out=hb_[hb2][:], in_=hS.rearrange("c p t -> p c t")[:, :, a1:a1 + 286]),
                        reads=[("hS", j, ti) for j in range(8) for ti in range(max(0, a1 // 512), min(NT1 // 512, (a1 + 285) // 512 + 1))],
                        writes=[("ah", hb2)]))
                    def cstep(c):
                        pb = c % 4

                        def fc(e, c=c, pb=pb, hb2=hb2):
                            for k in range(31):
                                r = e.matmul(pcv[pb][:, 0:256], lhsT=diagw[:, c, k, :], rhs=hb_[hb2][:, c, k:k + 256], start=(k == 0), stop=(k == 30))
                            return r
                        P.op("pe", fc, reads=[("ah", hb2), ("adiag", c, 0), ("adiag", c, 1)], writes=[("apcv", pb)])
                        P.op("act", lambda e, c=c, pb=pb: e.activation(out=accA[:, c, :], in_=pcv[pb][:, 0:256], func=AF.Identity, bias=b_dw[:, c:c + 1]),
                             reads=[("apcv", pb)] + par, writes=[("accA", hb2, c)])
                    for c in range(8):
                        steps.append(lambda c=c: cstep(c))
                    return steps

                class Rec:
                    def __init__(self):
                        self.items = []

                    def op(self, *a, **k):
                        self.items.append(lambda: P.op(*a, **k))

                    def dma(self, *a, **k):
                        self.items.append(lambda: P.dma(*a, **k))

                def p2a_chain(s):
                    R = Rec()
                    hb2 = s % 2
                    accA = accA2[hb2]
                    yc = yc2[hb2]
                    for c in range(8):
                        s2 = c % 2
                        R.op("act", lambda e, c=c, s2=s2: e.activation(out=sqt[s2][:], in_=accA[:, c, :], func=AF.Copy),
                             reads=[("accA", hb2, c)], writes=[("asq", s2)])
                        R.op("pe", lambda e, c=c, s2=s2: e.matmul(pSt[:, 0:256], lhsT=ones_bf[:], rhs=sqt[s2][:], start=(c == 0), stop=(c == 7)),
                             reads=[("asq", s2), "ones_bf"], writes=["apst0"])
                    for c in range(8):
                        s2 = c % 2
                        R.op("act", lambda e, c=c, s2=s2: e.activation(out=sqt[s2][:], in_=accA[:, c, :], func=AF.Square),
                             reads=[("accA", hb2, c)], writes=[("asq", s2)])
                        R.op("pe", lambda e, c=c, s2=s2: e.matmul(pSt[:, 256:512], lhsT=ones_bf[:], rhs=sqt[s2][:], start=(c == 0), stop=(c == 7)),
                             reads=[("asq", s2), "apst0", "ones_bf"], writes=["apst1"])
                    R.op("act", lambda e: e.activation(out=mean[:], in_=pSt[:, 0:256], func=AF.Copy, scale=1.0 / 1024),
                         reads=["apst0", "apst1"], writes=["amean"])
                    R.op("act", lambda e: e.activation(out=rs[:], in_=pSt[:, 256:512], func=AF.Copy, scale=1.0 / 1024),
                         reads=["apst1"], writes=["ars"])
                    R.op("dve", lambda e: e.tensor_tensor(out=msq[:], in0=mean[:], in1=mean[:], op=ALU.mult),
                         reads=["amean"], writes=["amsq"])
                    R.op("dve", lambda e: e.tensor_tensor(out=rs[:], in0=rs[:], in1=msq[:], op=ALU.subtract),
                         reads=["ars", "amsq"], writes=["ars"])
                    R.op("act", lambda e: e.activation(out=rs[:], in_=rs[:], func=AF.Sqrt, bias=EPS), reads=["ars"], writes=["ars"])
                    R.op("dve", lambda e: e.reciprocal(out=rs[:], in_=rs[:]), reads=["ars"], writes=["ars"])
                    R.op("dve", lambda e: e.tensor_tensor(out=nmr[:], in0=mean[:], in1=rs[:], op=ALU.mult),
                         reads=["amean", "ars"], writes=["anmr"])
                    for c in range(8):
                        eng = "dve" if c % 2 == 0 else "pool"
                        R.op(eng, lambda e, c=c: e.tensor_tensor(out=accA[:, c, :], in0=accA[:, c, :], in1=rs[:], op=ALU.mult),
                             reads=[("accA", hb2, c), "ars"], writes=[("accA", hb2, c)])
                        R.op(eng, lambda e, c=c: e.tensor_tensor(out=accA[:, c, :], in0=accA[:, c, :], in1=nmr[:], op=ALU.subtract),
                             reads=[("accA", hb2, c), "anmr"], writes=[("accA", hb2, c)])
                        R.op("act", lambda e, c=c: e.activation(out=yc[:, c, :], in_=accA[:, c, :], func=(AF.Sigmoid if DBG.get("nosilu") else AF.Silu),
                                                                scale=lg[:, c:c + 1], bias=lb[:, c:c + 1]),
                             reads=[("accA", hb2, c)] + par, writes=[("ayc", hb2, c)])
                    for c in range(8):
                        s2 = c % 2
                        R.op("act", lambda e, c=c, s2=s2: e.activation(out=sqt[s2][:], in_=yc[:, c, :], func=AF.Square),
                             reads=[("ayc", hb2, c)], writes=[("asq", s2)])
                        R.op("pe", lambda e, c=c, s2=s2: e.matmul(pS2[:], lhsT=ones_bf[:], rhs=sqt[s2][:], start=(c == 0), stop=(c == 7)),
                             reads=[("asq", s2), "ones_bf"], writes=["aps2"])
                    R.op("act", lambda e: e.activation(out=rs2[:], in_=pS2[:], func=AF.Sqrt, scale=1.0 / 1024, bias=EPS),
                         reads=["aps2"], writes=["ars2"])
                    R.op("dve", lambda e: e.reciprocal(out=rs2[:], in_=rs2[:]), reads=["ars2"], writes=["ars2"])
                    mb = s % 2
                    for c in range(8):
                        eng = "dve"
                        R.op(eng, lambda e, c=c, mb=mb: e.scalar_tensor_tensor(
                            out=mo[mb][:, c, :], in0=yc[:, c, :], scalar=og[:, c:c + 1], in1=rs2[:], op0=ALU.mult, op1=ALU.mult),
                            reads=[("ayc", hb2, c), "ars2"] + par, writes=[("amo", mb, c)])
                    R.dma("pool", f"amo{mb}", lambda e, s=s, mb=mb: e.dma_start(
                        out=mcS.rearrange("c p t -> p c t")[:, :, 256 * s:256 * s + 256], in_=mo[mb][:]),
                        reads=[("amo", mb, c) for c in range(8)], writes=[("mcS", s)])
                    return R.items

                NA_ = DBG.get("p2a_tiles", NT2 // 256)
                for st_ in p2a_conv(0):
                    st_()
                for s in range(NA_):
                    cv = p2a_conv(s + 1) if s + 1 < NA_ else []
                    ch = p2a_chain(s)
                    every = max(1, len(ch) // (len(cv) + 1)) if cv else 10 ** 9
                    ic = 0
                    for i, it in enumerate(ch):
                        if cv and i % every == 0 and ic < len(cv):
                            cv[ic]()
                            ic += 1
                        it()
                    while ic < len(cv):
                        cv[ic]()
                        ic += 1
                    if "p0" in phases:
                        cc_up.emit(3)
                if "p0" in phases:
                    cc_up.flush()
                P.barrier(dram_slots=("cast_win16", "cast_wout16", "cast_wup16", "cast_wdn16"))

        if "p2b" in phases:
            with contextlib.ExitStack() as st:
                qe = [sb(st, f"bqe{i}", [128, 8, 256], BF16) for i in range(2)]
                qo = [sb(st, f"bqo{i}", [128, 8, 256], BF16) for i in range(2)]
                RING = 10
                kb = sb(st, "bk", [128, 8, RING * 128], BF16)
                vb_ = sb(st, "bv", [128, RING, 1040], BF16)
                wres = sb(st, "bwres", [128, 8, 16, 128], BF16)
                bt = sb(st, "bbt", [128, 16, 768], BF16)
                pt = [sb(st, f"bpt{i}", [128, 768], BF16) for i in range(2)]
                yna = [sb(st, f"byna{i}", [128, 1024], F32) for i in range(2)]
                ysq = sb(st, "bysq", [128, 1024], BF16)
                ssq = sb(st, "bssq", [128, 1], F32)
                rden = [sb(st, f"brden{i}", [128, 2], F32) for i in range(2)]
                yn = sb(st, "byn", [128, 1024], BF16)
                gna_s = sb(st, "bgna", [128, 1024], F32)
                mx = [sb(st, f"bmx{i}", [128, 16, 256], BF16) for i in range(2)]
                NWO = 2
                wo = [sb(st, f"bwo{i}", [128, 16, 128], BF16) for i in range(NWO)]
                xc = [sb(st, f"bxc{i}", [128, 256], F32) for i in range(3)]
                x1 = sb(st, "bx1", [128, 16, 256], F32)
                sqb = [sb(st, f"bsqb{i}", [128, 256], BF16) for i in range(2)]
                rm = sb(st, "brm", [128, 256], F32)
                mk = [sb(st, f"bmk{i}", [128, 256], F32) for i in range(2)]
                xn2 = sb(st, "bxn2", [128, 16, 256], BF16)
                gf = sb(st, "bgf", [128, 16], F32)
                psS = ps(st, "bpsS", [128, 2048])
                psO = ps(st, "bpsO", [128, 1024])
                psXX = ps(st, "bpsXX", [128, 1024])
                psX = [psXX[:, 0:512], psXX[:, 512:1024]]
                psT = psXX[:, 512:1024].bitcast(BF16)
                sqacc = sb(st, "bsqacc", [128, 256], F32)
                sqtmp = [sb(st, f"bsqtmp{i}", [128, 256], F32) for i in range(2)]
                sqlo = sb(st, "bsqlo", [128, 256], F32)
                P.dma("sp", "bgna", lambda e: e.dma_start(out=gna_s[:], in_=gna), writes=["bgna"])
                P.dma("sp", "bgf", lambda e: e.dma_start(out=gf[:], in_=g_ffn), writes=["bgf"])
                for i in range(2):
                    P.op("dve", lambda e, i=i: e.memset(qe[i][:], 0.0), writes=[("bqe", i)])
                    P.op("pool", lambda e, i=i: e.memset(qo[i][:], 0.0), writes=[("bqo", i)])
                wcnt = [0]
                cur_tab = [None]
                NS = DBG.get("p2b_tiles", NT2 // 256)
                qS3 = qS.rearrange("c p t -> p c t")
                kS3 = kS.rearrange("c p t -> p c t")
                mcS3 = mcS.rearrange("c p t -> p c t")

                def load_tab(kind):
                    if cur_tab[0] == kind:
                        return
                    cur_tab[0] = kind
                    for hh in range(0, 16, 4):
                        P.dma("pool", "bbt", lambda e, kind=kind, hh=hh: e.dma_start(out=bt[:, hh:hh + 4, :], in_=btab[kind, :, hh:hh + 4, :]),
                              writes=["bbt"])

                def loads(s):
                    a = 256 * s
                    par = s % 2
                    qr = [("qS", j, ti) for j in range(8) for ti in range((a + 384) // 512, (a + 639) // 512 + 1)]
                    P.dma("sp", f"bqe{par}", lambda e: e.dma_start(out=qe[par][0:64, :, :], in_=qS3[0:64, :, a + 384:a + 640]),
                          reads=qr, writes=[("bqe", par)])
                    P.dma("sp", f"bqo{par}", lambda e: e.dma_start(out=qo[par][64:128, :, :], in_=qS3[64:128, :, a + 384:a + 640]),
                          reads=qr, writes=[("bqo", par)])
                    P.dma("sp", f"bmxc{par}", lambda e: e.dma_start(out=mx[par][:, 0:8, :], in_=mcS3[:, :, 256 * s:256 * s + 256]),
                          reads=[("mcS", s)], writes=[("bmx", par, "c")])
                    P.dma("sp", f"bmk{par}", lambda e: e.dma_start(out=mk[par][:], in_=tmask[:, 256 * s:256 * s + 256]), writes=[("bmk", par)])

                def load_kv(g):
                    if g + 1 >= NT1 // 128:
                        return
                    sl = g % RING
                    sp_ = sl // 2
                    ti = (g * 128) // 512
                    P.dma("sp", f"bk{sp_}", lambda e: e.dma_start(out=kb[:, :, sl * 128:(sl + 2) * 128], in_=kS3[:, :, g * 128:(g + 2) * 128]),
                          reads=[("kS", j, ti) for j in range(8)], writes=[("bk", sp_)])
                    P.dma("sp", f"bv{sp_}", lambda e: e.dma_start(out=vb_[:, sl:sl + 2, :], in_=vS[g * 128:(g + 2) * 128, :].rearrange("(c p) f -> p c f", p=128)),
                          reads=[("vS", ti)], writes=[("bv", sp_)])

                def na_steps(s):
                    par = s % 2
                    steps = []

                    def pair(pi):
                        pp = 2 * s - 1 + pi
                        if pp <= 0:
                            kind, nch, c0 = 0, 6, 1 + pi
                        elif pp == 1:
                            kind, nch, c0 = 1, 5, 1 + pi
                        elif pp <= 29:
                            kind, nch, c0 = 2, 5, 1 + pi
                        elif pp == 30:
                            kind, nch, c0 = 3, 5, 1 + pi
                        else:
                            kind, nch, c0 = 4, 6, 0 + pi
                        yb = pi
                        qoff = pi * 128

                        def slot(ci):
                            return (2 * s + c0 + ci) % RING
                        kkeys = sorted(set(("bk", slot(ci) // 2) for ci in range(nch)))
                        vkeys = sorted(set(("bv", slot(ci) // 2) for ci in range(nch)))

                        def s_mm(h):
                            sbuf_i = h % 2
                            hp = h // 2
                            qsrc = qe[par] if h % 2 == 0 else qo[par]
                            base = sbuf_i * 1024

                            def f(e):
                                for ci in range(nch):
                                    e.matmul(psS[:, base + ci * 128: base + (ci + 1) * 128],
                                             lhsT=kb[:, hp, slot(ci) * 128:(slot(ci) + 1) * 128],
                                             rhs=qsrc[:, hp, qoff:qoff + 128], start=True, stop=False)
                                    r = e.matmul(psS[:, base + ci * 128: base + (ci + 1) * 128],
                                                 lhsT=ident[:], rhs=bt[:, h, ci * 128:(ci + 1) * 128], start=False, stop=True)
                                return r
                            P.op("pe", f, reads=kkeys + [("bqe", par), ("bqo", par), "bbt", "ident"], writes=[("psS", sbuf_i)])
                            P.op("act", lambda e: e.activation(out=pt[sbuf_i][:, 0:nch * 128], in_=psS[:, base:base + nch * 128], func=AF.Exp),
                                 reads=[("psS", sbuf_i)], writes=[("bpt", sbuf_i)])

                        def pv_mm(h):
                            sbuf_i = h % 2
                            hp = h // 2
                            ob = hp % 2
                            obase = ob * 512 + (h % 2) * 65

                            def f(e):
                                for ci in range(nch):
                                    r = e.matmul(psO[:, obase:obase + 65], lhsT=pt[sbuf_i][:, ci * 128:(ci + 1) * 128],
                                                 rhs=vb_[:, slot(ci), h * 65:(h + 1) * 65], start=(ci == 0), stop=(ci == nch - 1))
                                return r
                            P.op("pe", f, reads=[("bpt", sbuf_i)] + vkeys, writes=[("psO", ob, 0), ("psO", ob, 1)])
                            if h % 2 == 1:
                                ob0 = ob * 512
                                ov = psO[:, ob0:ob0 + 130].rearrange("p (h d) -> p h d", d=65)
                                P.op("dve", lambda e: e.reciprocal(out=rden[ob][:].rearrange("p (h o) -> p h o", o=1), in_=ov[:, :, 64:65]),
                                     reads=[("psO", ob, 0), ("psO", ob, 1)], writes=[("brden", ob)])
                                for hh in range(2):
                                    P.op("dve", lambda e, hh=hh: e.scalar_tensor_tensor(
                                        out=yna[yb][:, hp * 128 + hh * 64: hp * 128 + hh * 64 + 64],
                                        in0=psO[:, ob0 + hh * 65: ob0 + hh * 65 + 64], scalar=rden[ob][:, hh:hh + 1], in1=ones_f[:, 0:64],
                                        op0=ALU.mult, op1=ALU.mult),
                                        reads=[("psO", ob, hh), ("brden", ob), "ones_f"], writes=[("byna", yb, hp)])

                        def first():
                            load_tab(kind)
                            s_mm(0)
                        steps.append(first)
                        for h in range(16):
                            def hstep(h=h):
                                if h + 1 < 16:
                                    s_mm(h + 1)
                                pv_mm(h)
                            steps.append(hstep)

                        def fin():
                            ynr = [("byna", yb, hp) for hp in range(8)]
                            P.op("act", lambda e: e.activation(out=ysq[:], in_=yna[yb][:], func=AF.Square, accum_out=ssq[:]),
                                 reads=ynr, writes=["bysq", "bssq"])
                            P.op("act", lambda e: e.activation(out=ssq[:], in_=ssq[:], func=AF.Sqrt, scale=1.0 / 1024, bias=EPS),
                                 reads=["bssq"], writes=["bssq"])
                            P.op("dve", lambda e: e.reciprocal(out=ssq[:], in_=ssq[:]), reads=["bssq"], writes=["bssq"])
                            P.op("dve", lambda e: e.scalar_tensor_tensor(out=yn[:], in0=yna[yb][:], scalar=ssq[:, 0:1], in1=gna_s[:],
                                                                         op0=ALU.mult, op1=ALU.mult),
                                 reads=ynr + ["bssq", "bgna"], writes=["byn"])

                            def ft(e):
                                for j in range(8):
                                    r = e.transpose(out=psT[:, j * 128:(j + 1) * 128], in_=yn[:, j * 128:(j + 1) * 128], identity=ident[:])
                                return r
                            P.op("pe", ft, reads=["byn", "ident"], writes=[("bpsX", 1)])
                            P.op("dve", lambda e: e.tensor_copy(
                                out=mx[par][:, 8:16, qoff:qoff + 128], in_=psT.rearrange("p (j t) -> p j t", t=128)),
                                reads=[("bpsX", 1)], writes=[("bmx", par, "n", pi)])
                        steps.append(fin)
                    pair(0)
                    pair(1)
                    return steps

                def outproj_steps(s):
                    a = 256 * s
                    par = s % 2
                    mxr = [("bmx", par, "c"), ("bmx", par, "n", 0), ("bmx", par, "n", 1)]
                    steps = []

                    def dcstep(dc):
                        if dc < 8:
                            wl = wres[:, dc, :, :]
                            wkey = ("bwres", dc)
                        else:
                            w = wcnt[0] % NWO
                            wcnt[0] += 1
                            P.dma("sp", f"bwo{w}", lambda e: e.dma_start(out=wo[w][:], in_=wout16[dc]),
                                  reads=[("wout16", dc)], writes=[("bwo", w)])
                            wl = wo[w]
                            wkey = ("bwo", w)
                        xb = dc % 3
                        P.dma("sp", f"bxc{xb}", lambda e: e.dma_start(out=xc[xb][:], in_=xT3[:, dc, a + 384:a + 640]),
                              writes=[("bxc", xb)])
                        pb = 0

                        def f(e):
                            for j in range(16):
                                r = e.matmul(psX[pb][:, 0:256], lhsT=wl[:, j, :], rhs=mx[par][:, j, :], start=(j == 0), stop=(j == 15))
                            return r
                        P.op("pe", f, reads=[wkey] + mxr, writes=[("bpsX", 0)])
                        P.op("dve", lambda e: e.tensor_tensor(out=x1[:, dc, :], in0=psX[pb][:, 0:256], in1=xc[xb][:], op=ALU.add),
                             reads=[("bpsX", 0), ("bxc", xb)], writes=[("bx1", dc)])
                        s2 = dc % 2
                        P.op("act", lambda e: e.activation(out=(sqacc[:] if dc == 0 else sqtmp[s2][:]), in_=x1[:, dc, :], func=AF.Square),
                             reads=[("bx1", dc)], writes=(["bsqacc"] if dc == 0 else [("bsqtmp", s2)]))
                        if dc > 0:
                            P.op("pool", lambda e: e.tensor_tensor(out=sqacc[:], in0=sqacc[:], in1=sqtmp[s2][:], op=ALU.add),
                                 reads=["bsqacc", ("bsqtmp", s2)], writes=["bsqacc"])
                    for dc in range(16):
                        steps.append(lambda dc=dc: dcstep(dc))
                    return steps

                def finalize(s):
                    par = s % 2
                    P.dma("pool", "bx1o", lambda e: e.dma_start(out=x1S.rearrange("c p t -> p c t")[:, :, 256 * s:256 * s + 256], in_=x1[:]),
                          reads=[("bx1", dc) for dc in range(16)], writes=[("x1S", s)])
                    P.op("act", lambda e: e.activation(out=sqb[0][:], in_=sqacc[:], func=AF.Copy), reads=["bsqacc"], writes=[("bsqb", 0)])
                    P.op("dve", lambda e: e.tensor_tensor(out=sqlo[:], in0=sqacc[:], in1=sqb[0][:], op=ALU.subtract),
                         reads=["bsqacc", ("bsqb", 0)], writes=["bsqlo"])
                    P.op("act", lambda e: e.activation(out=sqb[1][:], in_=sqlo[:], func=AF.Copy), reads=["bsqlo"], writes=[("bsqb", 1)])

                    def fs(e):
                        e.matmul(psX[0][:, 0:256], lhsT=ones_bf[:], rhs=sqb[0][:], start=True, stop=False)
                        return e.matmul(psX[0][:, 0:256], lhsT=ones_bf[:], rhs=sqb[1][:], start=False, stop=True)
                    P.op("pe", fs, reads=[("bsqb", 0), ("bsqb", 1), "ones_bf"], writes=[("bpsX", 0)])
                    P.op("act", lambda e: e.activation(out=rm[:], in_=psX[0][:, 0:256], func=AF.Sqrt, scale=1.0 / D, bias=EPS),
                         reads=[("bpsX", 0)], writes=["brm"])
                    P.op("dve", lambda e: e.reciprocal(out=rm[:], in_=rm[:]), reads=["brm"], writes=["brm"])
                    P.op("dve", lambda e: e.tensor_tensor(out=rm[:], in0=rm[:], in1=mk[par][:], op=ALU.mult), reads=["brm", ("bmk", par)], writes=["brm"])
                    for dc in range(16):
                        P.op("dve", lambda e, dc=dc: e.scalar_tensor_tensor(out=xn2[:, dc, :], in0=x1[:, dc, :], scalar=gf[:, dc:dc + 1], in1=rm[:],
                                                                            op0=ALU.mult, op1=ALU.mult),
                             reads=[("bx1", dc), "brm", "bgf"], writes=[("bxn2", dc)])
                    P.dma("pool", "bxn2o", lambda e: e.dma_start(out=xn2S.rearrange("c p t -> p c t")[:, :, 256 * s:256 * s + 256], in_=xn2[:]),
                          reads=[("bxn2", dc) for dc in range(16)], writes=[("xn2S", s)])

                for g in range(0, 10, 2):
                    load_kv(g)
                loads(0)
                for dc in range(8):
                    if dc >= 2:
                        P.wait("sp", P.bufs[("bwres", dc - 2)][0])
                    P.dma("sp", f"bwres{dc % 2}", lambda e, dc=dc: e.dma_start(out=wres[:, dc, :, :], in_=wout16[dc]),
                          reads=[("wout16", dc)], writes=[("bwres", dc)])
                for stp in na_steps(0):
                    stp()
                for s in range(NS):
                    if s + 2 < NS:
                        load_kv(2 * s + 10)
                    if s + 1 < NS:
                        loads(s + 1)
                        na = na_steps(s + 1)
                    else:
                        na = []
                    op_ = outproj_steps(s)
                    ia = 0
                    for st_ in op_:
                        for _ in range(3):
                            if ia < len(na):
                                na[ia]()
                                ia += 1
                        st_()
                    while ia < len(na):
                        na[ia]()
                        ia += 1
                    finalize(s)
                    if "p0" in phases:
                        cc_dn.emit(1)
                if "p0" in phases:
                    cc_dn.flush()
                P.barrier(dram_slots=("cast_win16", "cast_wout16", "cast_wup16", "cast_wdn16"))

        out_toks = []
        if "p3" in phases:
            with contextlib.ExitStack() as st:
                WM = 458
                xt = [sb(st, f"cxt{i}", [128, 16, WM], BF16) for i in range(2)]
                G = sb(st, "cG", [128, 44, 456], BF16)
                NWU = 3
                wu = [sb(st, f"cwu{i}", [128, 16, 256], BF16) for i in range(NWU)]
                wd = [sb(st, f"cwd{i}", [128, 44, 128], BF16) for i in range(2)]
                cg = [sb(st, f"ccg{i}", [128, 456], F32) for i in range(2)]
                cv = [sb(st, f"ccv{i}", [128, 456], F32) for i in range(2)]
                sg = [sb(st, f"csg{i}", [128, 456], F32) for i in range(2)]
                x2 = sb(st, "cx2", [128, 16, 456], F32)
                x1c = [sb(st, f"cx1c{i}", [128, 456], F32) for i in range(3)]
                sqc = [sb(st, f"csq{i}", [128, 456], BF16) for i in range(2)]
                rf = sb(st, "crf", [128, 456], F32)
                oc = [sb(st, f"coc{i}", [128, 456], F32) for i in range(3)]
                fw = sb(st, "cfw", [128, 88, 3], F32)
                fb = sb(st, "cfb", [128, 88], F32)
                gfin = sb(st, "cgfin", [128, 16], F32)
                pg = [ps(st, f"cpg{i}", [128, 512]) for i in range(2)]
                pv = [ps(st, f"cpv{i}", [128, 512]) for i in range(2)]
                po = [ps(st, f"cpo{i}", [128, 512]) for i in range(2)]
                pq = ps(st, "cpq", [128, 512])
                P.dma("sp", "cfw", lambda e: e.dma_start(out=fw[:], in_=fdw), writes=["cfw"])
                P.dma("sp", "cfb", lambda e: e.dma_start(out=fb[:], in_=fdb), writes=["cfb"])
                P.dma("sp", "cgfin", lambda e: e.dma_start(out=gfin[:], in_=g_fin), writes=["cgfin"])
                par = ["cfw", "cfb"]
                wuc = [0]
                wdc = [0]
                def p3_tile(ti, Wd, t0):
                    WN = Wd + 2
                    a2 = t0 - 1 + OFF2
                    xb = ti % 2
                    P.dma("sp", f"cxt{xb}", lambda e, a2=a2, WN=WN, xb=xb: e.dma_start(
                        out=xt[xb][:, :, 0:WN], in_=xn2S.rearrange("c p t -> p c t")[:, :, a2:a2 + WN]),
                        reads=[("xn2S", s) for s in range(a2 // 256, (a2 + WN - 1) // 256 + 1)], writes=[("cxt", xb)])
                    def p3_up(j):
                        w = wuc[0] % NWU
                        wuc[0] += 1
                        P.dma("sp", f"cwu{w}", lambda e, j=j, w=w: e.dma_start(out=wu[w][:], in_=wup16[j]),
                              reads=[("wup16", j)], writes=[("cwu", w)])
                        pb = j % 2
                        for half, (pt_, pkey) in enumerate(((pg[pb], ("cpg", pb)), (pv[pb], ("cpv", pb)))):
                            def f(e, w=w, pt_=pt_, half=half, xb=xb, WN=WN):
                                for c in range(16):
                                    r = e.matmul(pt_[:, 0:WN], lhsT=wu[w][:, c, half * 128:(half + 1) * 128], rhs=xt[xb][:, c, 0:WN],
                                                 start=(c == 0), stop=(c == 15))
                                return r
                            P.op("pe", f, reads=[("cwu", w), ("cxt", xb)], writes=[pkey])
                        for half, (pt_, pkey, dst, dkey) in enumerate(((pg[pb], ("cpg", pb), cg[pb], ("ccg", pb)),
                                                                         (pv[pb], ("cpv", pb), cv[pb], ("ccv", pb)))):
                            ch = j + 44 * half
                            P.op("act", lambda e, pt_=pt_, dst=dst, ch=ch, Wd=Wd: e.activation(
                                out=dst[:, 0:Wd], in_=pt_[:, 2:Wd + 2], func=AF.Identity, scale=fw[:, ch, 2:3], bias=fb[:, ch:ch + 1]),
                                reads=[pkey] + par, writes=[dkey])
                            P.op("dve", lambda e, pt_=pt_, dst=dst, ch=ch, Wd=Wd: e.scalar_tensor_tensor(
                                out=dst[:, 0:Wd], in0=pt_[:, 1:Wd + 1], scalar=fw[:, ch, 1:2], in1=dst[:, 0:Wd], op0=ALU.mult, op1=ALU.add),
                                reads=[pkey, dkey] + par, writes=[dkey])
                        for half, (pt_, pkey, dst, dkey) in enumerate(((pg[pb], ("cpg", pb), cg[pb], ("ccg", pb)),
                                                                         (pv[pb], ("cpv", pb), cv[pb], ("ccv", pb)))):
                            ch = j + 44 * half
                            P.op("dve", lambda e, pt_=pt_, dst=dst, ch=ch, Wd=Wd: e.scalar_tensor_tensor(
                                out=dst[:, 0:Wd], in0=pt_[:, 0:Wd], scalar=fw[:, ch, 0:1], in1=dst[:, 0:Wd], op0=ALU.mult, op1=ALU.add),
                                reads=[pkey, dkey] + par, writes=[dkey])
                        P.op("act", lambda e, pb=pb, Wd=Wd: e.activation(out=sg[pb][:, 0:Wd], in_=cg[pb][:, 0:Wd], func=AF.Silu),
                             reads=[("ccg", pb)], writes=[("csg", pb)])
                        P.op("pool", lambda e, pb=pb, j=j, Wd=Wd: e.tensor_tensor(out=G[:, j, 0:Wd], in0=sg[pb][:, 0:Wd], in1=cv[pb][:, 0:Wd], op=ALU.mult),
                             reads=[("csg", pb), ("ccv", pb)], writes=[("cG", j)])
                    for j in range(44):
                        p3_up(j)
                    Gr = [("cG", j) for j in range(44)]
                    def p3_dn(dc):
                        w = wdc[0] % 2
                        wdc[0] += 1
                        P.dma("sp", f"cwd{w}", lambda e, dc=dc, w=w: e.dma_start(out=wd[w][:], in_=wdn16[dc]),
                              reads=[("wdn16", dc)], writes=[("cwd", w)])
                        xb3 = dc % 3
                        P.dma("sp", f"cx1c{xb3}", lambda e, dc=dc, xb3=xb3, t0=t0, Wd=Wd: e.dma_start(
                            out=x1c[xb3][:, 0:Wd], in_=x1S[dc, :, t0 + OFF2:t0 + OFF2 + Wd]),
                            reads=[("x1S", s) for s in range((t0 + OFF2) // 256, (t0 + OFF2 + Wd - 1) // 256 + 1)], writes=[("cx1c", xb3)])
                        pb = dc % 2

                        def f(e, w=w, pb=pb, Wd=Wd):
                            for j in range(44):
                                r = e.matmul(po[pb][:, 0:Wd], lhsT=wd[w][:, j, :], rhs=G[:, j, 0:Wd], start=(j == 0), stop=(j == 43))
                            return r
                        P.op("pe", f, reads=[("cwd", w)] + Gr, writes=[("cpo", pb)])
                        P.op("dve", lambda e, dc=dc, pb=pb, xb3=xb3, Wd=Wd: e.tensor_tensor(
                            out=x2[:, dc, 0:Wd], in0=po[pb][:, 0:Wd], in1=x1c[xb3][:, 0:Wd], op=ALU.add),
                            reads=[("cpo", pb), ("cx1c", xb3)], writes=[("cx2", dc)])
                        s2 = dc % 2
                        P.op("act", lambda e, dc=dc, s2=s2, Wd=Wd: e.activation(out=sqc[s2][:, 0:Wd], in_=x2[:, dc, 0:Wd], func=AF.Square),
                             reads=[("cx2", dc)], writes=[("csq", s2)])
                        P.op("pe", lambda e, dc=dc, s2=s2, Wd=Wd: e.matmul(pq[:, 0:Wd], lhsT=ones_bf[:], rhs=sqc[s2][:, 0:Wd], start=(dc == 0), stop=(dc == 15)),
                             reads=[("csq", s2), "ones_bf"], writes=["cpq"])
                    for dc in range(16):
                        p3_dn(dc)
                    P.op("act", lambda e, Wd=Wd: e.activation(out=rf[:, 0:Wd], in_=pq[:, 0:Wd], func=AF.Sqrt, scale=1.0 / D, bias=EPS),
                         reads=["cpq"], writes=["crf"])
                    P.op("dve", lambda e, Wd=Wd: e.reciprocal(out=rf[:, 0:Wd], in_=rf[:, 0:Wd]), reads=["crf"], writes=["crf"])
                    for dc in range(16):
                        ob = dc % 3
                        eng = "dve"
                        P.op(eng, lambda e, dc=dc, ob=ob, Wd=Wd: e.scalar_tensor_tensor(
                            out=oc[ob][:, 0:Wd], in0=x2[:, dc, 0:Wd], scalar=gfin[:, dc:dc + 1], in1=rf[:, 0:Wd], op0=ALU.mult, op1=ALU.mult),
                            reads=[("cx2", dc), "crf", "cgfin"], writes=[("coc", ob)])
                        tk = P.dma("sp", f"coc{ob}", lambda e, dc=dc, ob=ob, t0=t0, Wd=Wd: e.dma_start(
                            out=yT[dc * 128:(dc + 1) * 128, t0:t0 + Wd], in_=oc[ob][:, 0:Wd]),
                            reads=[("coc", ob)], writes=[("yT", dc, ti)])
                        out_toks.append(tk)
                t0 = 0
                for ti, Wd in enumerate(P3W[:DBG.get("p3_tiles", 9)]):
                    p3_tile(ti, Wd, t0)
                    t0 += Wd
                P.barrier(dram_slots=("cast_win16", "cast_wout16", "cast_wup16", "cast_wdn16"))
        P.barrier(dram_slots=("cast_win16", "cast_wout16", "cast_wup16", "cast_wdn16"))
        for s in P.slots.values():
            if s[1] > 0:
                P.wait("sp", (s[0], s[1]))
        if DBG.get("verbose"):
            print("nsem", P.nsem, {e: P.cnt[e] for e in ENGS})
        P.replay()
    return nc


def _fm(v, nch):
    return np.ascontiguousarray(v.reshape(nch, 128).T)


def _bias_tables(rpb, core):
    q = core % 4
    R0 = q * 64
    rows = SEQ // 64
    out = np.full((5, 128, 16, 768), NEG, np.float32)
    kinds = [(0, -4, 6), (1, -4, 5), (2, -4, 5), (30, -4, 5), (31, -6, 6)]
    kp = np.arange(128)
    k_par, k_col = kp // 64, kp % 64
    qp = np.arange(128)
    q_par, q_col = qp // 64, qp % 64
    cs = np.clip(q_col - 8, 0, 64 - 16)
    for ki, (pp, rel, nch) in enumerate(kinds):
        for ci in range(nch):
            rq = R0 + 2 * pp + q_par
            rk = R0 + 2 * pp + rel + 2 * ci + k_par
            fake = (rq < 0) | (rq >= rows)
            rs = np.where(fake, rq - 4, np.clip(rq - 4, 0, rows - 8))
            dr = rk[:, None] - rq[None, :] + 7
            okr = (rk[:, None] >= rs[None, :]) & (rk[:, None] < rs[None, :] + 8)
            okr &= (fake[None, :] | ((rk[:, None] >= 0) & (rk[:, None] < rows)))
            dc = k_col[:, None] - q_col[None, :] + 15
            okc = (k_col[:, None] >= cs[None, :]) & (k_col[:, None] < cs[None, :] + 16)
            ok = okr & okc
            drc = np.clip(dr, 0, 14)
            dcc = np.clip(dc, 0, 30)
            vals = rpb[:, drc, dcc]
            blk = np.where(ok[None], vals, np.float32(NEG)).astype(np.float32)
            out[ki, :, :, ci * 128:(ci + 1) * 128] = blk.transpose(1, 0, 2)
    return out


def prep_inputs(x, attn_norm_g, w_in, conv_dw_w, conv_dw_b, conv_ln_g, conv_ln_b, rpb, conv_out_g, na_out_g,
                w_out, ffn_norm_g, w_up, ffn_dw_w, ffn_dw_b, w_down, final_norm_g):
    f = np.float32
    x = np.asarray(x, f)
    w_in = np.asarray(w_in, f)[0]
    w_out = np.asarray(w_out, f)[0]
    w_up = np.asarray(w_up, f)[0]
    w_down = np.asarray(w_down, f)[0]
    win = np.ascontiguousarray(w_in.reshape(16, 128, 40, 128).transpose(2, 1, 0, 3))
    wout = np.ascontiguousarray(w_out.reshape(16, 128, 16, 128).transpose(2, 1, 0, 3))
    wg = w_up[:, :DFF].reshape(16, 128, 44, 128).transpose(2, 1, 0, 3)
    wv = w_up[:, DFF:].reshape(16, 128, 44, 128).transpose(2, 1, 0, 3)
    wup = np.ascontiguousarray(np.concatenate([wg, wv], axis=3))
    wdn = np.ascontiguousarray(w_down.reshape(44, 128, 16, 128).transpose(2, 1, 0, 3))
    shared = {
        "win": win, "wout": wout, "wup": wup, "wdn": wdn,
        "g_attn": _fm(np.asarray(attn_norm_g, f)[0], 16),
        "cdw": np.ascontiguousarray(np.asarray(conv_dw_w, f)[0].T.reshape(8, 128, 31).transpose(1, 0, 2)),
        "cdb": _fm(np.asarray(conv_dw_b, f)[0], 8),
        "clg": _fm(np.asarray(conv_ln_g, f)[0], 8),
        "clb": _fm(np.asarray(conv_ln_b, f)[0], 8),
        "cog": _fm(np.asarray(conv_out_g, f)[0], 8),
        "gna": np.ascontiguousarray(np.broadcast_to(np.asarray(na_out_g, f)[0][None, :], (128, 1024))),
        "g_ffn": _fm(np.asarray(ffn_norm_g, f)[0], 16),
        "fdw": np.ascontiguousarray(np.asarray(ffn_dw_w, f)[0].T.reshape(88, 128, 3).transpose(1, 0, 2)),
        "fdb": _fm(np.asarray(ffn_dw_b, f)[0], 88),
        "g_fin": _fm(np.asarray(final_norm_g, f), 16),
        "ident_in": np.eye(128, dtype=f),
    }
    rp = np.asarray(rpb, f)[0]
    maps = []
    for c in range(NCORE):
        b, q = c // 4, c % 4
        T0 = q * TPC
        lo, hi = T0 - OFF1, T0 - OFF1 + NT1
        xt = np.zeros((D, NT1), f)
        slo, shi = max(lo, 0), min(hi, SEQ)
        xt[:, slo - lo:shi - lo] = x[b, slo:shi, :].T
        tm = np.zeros((NT2,), f)
        lo2 = T0 - OFF2
        s2lo, s2hi = max(lo2, 0), min(lo2 + NT2, SEQ)
        tm[s2lo - lo2:s2hi - lo2] = 1.0
        m = dict(shared)
        m["xT"] = xt
        m["btab"] = _bias_tables(rp, c)
        m["tmask"] = np.ascontiguousarray(np.broadcast_to(tm[None, :], (128, NT2)))
        maps.append(m)
    return maps


def kernel(**inputs):
    maps = prep_inputs(**inputs)
    nc = build()
    res = run_bass_kernel_spmd(nc, maps, core_ids=list(range(NCORE)))
    out = np.empty((2, SEQ, D), np.float32)
    for c in range(NCORE):
        b, q = c // 4, c % 4
        out[b, q * TPC:(q + 1) * TPC, :] = res.results[c]["yT"].T
    return out
```
